# Optimizing a Trainium2 kernel written in Bass

```python
import jax, jax.numpy as jnp
from jax import lax
import numpy as np

D_MODEL = 1024
BATCH = 16
SEQ = 256
DEPTH = 4
DEC_BATCH = 4
DEC_SEQ = 2048
PAST_LEN = 512

GRID_W = 64
HEAD_DIM = 128
A_HEADS = 4
A_WIDTH = A_HEADS * HEAD_DIM
CHUNK = 64
CONV_K = 3
B_HEADS = 4
B_KV_HEADS = 2
B_WIDTH = B_HEADS * HEAD_DIM
WINDOW = 128
C_HEADS = 8
C_KV_HEADS = 2
C_WIDTH = C_HEADS * HEAD_DIM
BLOCK = 128
D_FF = 4 * D_MODEL
ROT_HALF = HEAD_DIM // 2
ROT_FREQS = ROT_HALF // 2
ROPE_THETA = 10000.0
EPS = 1e-6
NEG = -1e30
N_EVEN = (DEPTH + 1) // 2
N_ODD = DEPTH // 2
EVEN_IN = 4 * A_WIDTH + 4 * A_HEADS + B_WIDTH + 2 * B_KV_HEADS * HEAD_DIM
EVEN_SPLITS = (3 * A_WIDTH, 4 * A_WIDTH, 4 * A_WIDTH + 4 * A_HEADS, 4 * A_WIDTH + 4 * A_HEADS + B_WIDTH)
ODD_IN = C_WIDTH + 2 * C_KV_HEADS * HEAD_DIM
ODD_SPLITS = (C_WIDTH, C_WIDTH + C_KV_HEADS * HEAD_DIM)
F32 = jnp.float32

kernel_name = 'hybrid_deltanet_swa_qknorm_dit_step'


def _rms(x, g):
    xf = x.astype(F32)
    y = xf * lax.rsqrt(jnp.mean(xf * xf, axis=-1, keepdims=True) + EPS)
    return (y * g.astype(F32)).astype(x.dtype)


def _l2norm(x):
    return x * lax.rsqrt(jnp.sum(x * x, axis=-1, keepdims=True) + EPS)


def _adaln(cond, w, b):
    m = jax.nn.silu(cond) @ w + b
    return jnp.split(m[:, None, :], 6, axis=-1)


def _modulate(x, g, shift, scale):
    return _rms(x, g) * (1 + scale) + shift


def _mlp(h, w1, w2):
    return jnp.square(jax.nn.relu(h @ w1)) @ w2


def _short_conv(x, w):
    return lax.conv_general_dilated(x, w[:, None, :].astype(x.dtype), (1,), [(CONV_K // 2, CONV_K // 2)],
                                    dimension_numbers=('NWC', 'WIO', 'NWC'), feature_group_count=x.shape[-1])


def _axial_rope_tables(L):
    rows = L // GRID_W
    row = jnp.repeat(jnp.arange(rows, dtype=F32), GRID_W)
    col = jnp.tile(jnp.arange(GRID_W, dtype=F32), rows)
    inv = ROPE_THETA ** (-jnp.arange(ROT_FREQS, dtype=F32) / ROT_FREQS)
    ang = jnp.stack([row, col], axis=-1)[:, :, None] * inv
    return jnp.cos(ang)[:, None], jnp.sin(ang)[:, None]


def _rope(x, cos, sin):
    Bn, L, H, D = x.shape
    xr = x.reshape(Bn, L, H, 2, 2, ROT_FREQS).astype(F32)
    x1, x2 = xr[..., 0, :], xr[..., 1, :]
    out = jnp.stack([x1 * cos - x2 * sin, x2 * cos + x1 * sin], axis=-2)
    return out.reshape(Bn, L, H, D).astype(x.dtype)


def _group(q, n_kv):
    Bn, L, H, D = q.shape
    return q.reshape(Bn, L, n_kv, H // n_kv, D)


def _attn_blocked(q, k, v, sink):
    Bn, L, KV, G, D = q.shape
    nb = L // BLOCK
    qb = jnp.moveaxis(q.reshape(Bn, nb, BLOCK, KV, G, D), 1, 0)

    def one_block(qi):
        s = jnp.einsum('bqkgd,bmkd->bkgqm', qi, k, preferred_element_type=F32)
        if sink is not None:
            col = jnp.broadcast_to(sink.astype(F32)[None, :, :, None, None], s.shape[:-1] + (1,))
            p = jax.nn.softmax(jnp.concatenate([col, s], axis=-1), axis=-1)[..., 1:]
        else:
            p = jax.nn.softmax(s, axis=-1)
        return jnp.einsum('bkgqm,bmkd->bqkgd', p.astype(v.dtype), v)

    o = lax.map(one_block, qb)
    return jnp.moveaxis(o, 0, 1).reshape(Bn, L, KV * G * D)


def _banded_attn(q, k, v, ck, cv, sink):
    Bn, L, KV, G, D = q.shape
    nb = L // BLOCK
    P = ck.shape[1]
    qb = q.reshape(Bn, nb, BLOCK, KV, G, D)
    pad = lambda t: jnp.pad(t.reshape(Bn, nb, BLOCK, KV, D), ((0, 0), (1, 1), (0, 0), (0, 0), (0, 0)))
    band = lambda t: jnp.concatenate([t[:, :-2], t[:, 1:-1], t[:, 2:]], axis=2)
    kband, vband = band(pad(k)), band(pad(v))
    qpos = jnp.arange(nb)[:, None] * BLOCK + jnp.arange(BLOCK)[None, :]
    kpos = jnp.arange(nb)[:, None] * BLOCK + jnp.arange(-BLOCK, 2 * BLOCK)[None, :]
    valid = ((jnp.abs(qpos[:, :, None] - kpos[:, None, :]) <= WINDOW)
             & (kpos >= 0)[:, None, :] & (kpos < L)[:, None, :])
    s_loc = jnp.einsum('bnqkgd,bnmkd->bnkgqm', qb, kband, preferred_element_type=F32)
    s_loc = jnp.where(valid[None, :, None, None], s_loc, NEG)
    s_ctx = jnp.einsum('bnqkgd,bpkd->bnkgqp', qb, ck, preferred_element_type=F32)
    s_sink = jnp.broadcast_to(sink.astype(F32)[None, None, :, :, None, None], s_loc.shape[:-1] + (1,))
    p = jax.nn.softmax(jnp.concatenate([s_sink, s_ctx, s_loc], axis=-1), axis=-1).astype(v.dtype)
    o = (jnp.einsum('bnkgqp,bpkd->bnqkgd', p[..., 1:1 + P], cv)
         + jnp.einsum('bnkgqm,bnmkd->bnqkgd', p[..., 1 + P:], vband))
    return o.reshape(Bn, L, KV * G * D)


def _gdn_chunked(q, k, v, g, beta, s0):
    Bn, H, L, dk = q.shape
    dv = v.shape[-1]
    N = L // CHUNK
    q, k, v = [t.reshape(Bn, H, N, CHUNK, -1) for t in (q, k, v)]
    g = g.reshape(Bn, H, N, CHUNK)
    beta = beta.reshape(Bn, H, N, CHUNK)
    gc = jnp.cumsum(g, axis=-1)
    tril = jnp.tril(jnp.ones((CHUNK, CHUNK), bool))
    strict = jnp.tril(jnp.ones((CHUNK, CHUNK), bool), -1)
    diff = gc[..., :, None] - gc[..., None, :]
    decay = jnp.where(tril, jnp.exp(jnp.where(tril, diff, 0.0)), 0.0)
    kb = k * beta[..., None]
    lower = jnp.where(strict, jnp.einsum('bhncd,bhnsd->bhncs', kb, k) * decay, 0.0)
    a = lower + jnp.eye(CHUNK, dtype=F32)
    rhs = jnp.concatenate([v * beta[..., None], kb * jnp.exp(gc)[..., None]], axis=-1)
    sol = lax.linalg.triangular_solve(a, rhs, left_side=True, lower=True)
    u, w = sol[..., :dv], sol[..., dv:]
    qk = jnp.where(tril, jnp.einsum('bhncd,bhnsd->bhncs', q, k) * decay, 0.0)

    def step(S, xs):
        qi, ki, ui, wi, gi, qki = xs
        v_new = ui - jnp.einsum('bhcd,bhde->bhce', wi, S)
        o = (jnp.einsum('bhcd,bhde->bhce', qi * jnp.exp(gi)[..., None], S)
             + jnp.einsum('bhcs,bhse->bhce', qki, v_new))
        glast = gi[..., -1]
        S = (S * jnp.exp(glast)[..., None, None]
             + jnp.einsum('bhcd,bhce->bhde', ki * jnp.exp(glast[..., None] - gi)[..., None], v_new))
        return S, o

    xs = tuple(jnp.moveaxis(t, 2, 0) for t in (q, k, u, w, gc, qk))
    S, o = lax.scan(step, s0, xs)
    return jnp.moveaxis(o, 0, 2).reshape(Bn, H, L, dv), S


def _even_mixer(h, w_in, conv_w, a_log, dt_bias, norm_g, sink, w_out, ctx_state, ctx_kv):
    Bn, L, _ = h.shape
    qkv, gate, bg, qb, kvb = jnp.split(h @ w_in, EVEN_SPLITS, axis=-1)
    qkv = jax.nn.silu(_short_conv(qkv, conv_w)).astype(F32)
    qa, ka, va = [jnp.moveaxis(t.reshape(Bn, L, A_HEADS, HEAD_DIM), 2, 1) for t in jnp.split(qkv, 3, axis=-1)]
    qa = _l2norm(qa) * HEAD_DIM ** -0.5
    ka = _l2norm(ka)
    bg = bg.astype(F32).reshape(Bn, L, 2, 2, A_HEADS)
    beta = jnp.transpose(jax.nn.sigmoid(bg[:, :, 0]), (0, 2, 3, 1))
    g = -jnp.exp(a_log.astype(F32)) * jax.nn.softplus(bg[:, :, 1] + dt_bias.astype(F32))
    g = jnp.transpose(g, (0, 2, 3, 1))
    if ctx_state is None:
        s0 = jnp.zeros((Bn, 2, A_HEADS, HEAD_DIM, HEAD_DIM), F32)
    else:
        s0 = ctx_state.astype(F32)
    rev = lambda t: jnp.flip(t, axis=2)
    o_fwd, s_fwd = _gdn_chunked(qa, ka, va, g[:, 0], beta[:, 0], s0[:, 0])
    o_bwd, s_bwd = _gdn_chunked(rev(qa), rev(ka), rev(va), rev(g[:, 1]), rev(beta[:, 1]), s0[:, 1])
    o = jnp.moveaxis(o_fwd + rev(o_bwd), 1, 2)
    o = _rms(o, norm_g) * jax.nn.silu(gate.astype(F32).reshape(Bn, L, A_HEADS, HEAD_DIM))
    out_a = o.reshape(Bn, L, A_WIDTH).astype(h.dtype)
    state = jnp.stack([s_fwd, s_bwd], axis=1).astype(h.dtype)
    qb = qb.reshape(Bn, L, B_HEADS, HEAD_DIM)
    kb, vb = [t.reshape(Bn, L, B_KV_HEADS, HEAD_DIM) for t in jnp.split(kvb, 2, axis=-1)]
    kv_out = jnp.stack([kb, vb], axis=1)
    sink2 = sink.reshape(B_KV_HEADS, B_HEADS // B_KV_HEADS)
    if ctx_kv is None:
        out_b = _attn_blocked(_group(qb * HEAD_DIM ** -0.5, B_KV_HEADS), kb, vb, sink2)
    else:
        cos, sin = _axial_rope_tables(L)
        qr, kr = _rope(qb, cos, sin), _rope(kb, cos, sin)
        out_b = _banded_attn(_group(qr * HEAD_DIM ** -0.5, B_KV_HEADS), kr, vb, ctx_kv[:, 0], ctx_kv[:, 1], sink2)
    out = jnp.concatenate([out_a, out_b], axis=-1) @ w_out
    return out, state, kv_out


def _odd_mixer(h, w_in, q_g, k_g, w_out, ctx_kv):
    Bn, L, _ = h.shape
    q, k, v = jnp.split(h @ w_in, ODD_SPLITS, axis=-1)
    q = _rms(q.reshape(Bn, L, C_HEADS, HEAD_DIM), q_g)
    k = _rms(k.reshape(Bn, L, C_KV_HEADS, HEAD_DIM), k_g)
    v = v.reshape(Bn, L, C_KV_HEADS, HEAD_DIM)
    kv_out = jnp.stack([k, v], axis=1)
    if ctx_kv is not None:
        cos, sin = _axial_rope_tables(L)
        q, k = _rope(q, cos, sin), _rope(k, cos, sin)
        k = jnp.concatenate([ctx_kv[:, 0], k], axis=1)
        v = jnp.concatenate([ctx_kv[:, 1], v], axis=1)
    o = _attn_blocked(_group(q * HEAD_DIM ** -0.5, C_KV_HEADS), k, v, None)
    return o @ w_out, kv_out


def setup_inputs(seed: int = 0) -> dict:
    key = jax.random.key(seed)
    ks = jax.random.split(key, 32)
    nrm = lambda k, shape, s: jax.random.normal(k, shape, F32) * s
    dt = jnp.exp(jax.random.uniform(ks[14], (N_EVEN, 2, A_HEADS), F32, np.log(1e-3), np.log(1e-1)))
    return {
        'x_prompt': nrm(ks[0], (BATCH, SEQ, D_MODEL), 1.0),
        'x_sample': nrm(ks[1], (DEC_BATCH, DEC_SEQ, D_MODEL), 1.0),
        'state_a': nrm(ks[2], (DEC_BATCH, N_EVEN, 2, A_HEADS, HEAD_DIM, HEAD_DIM), 0.1),
        'cache_b_kv': nrm(ks[3], (DEC_BATCH, N_EVEN, 2, PAST_LEN, B_KV_HEADS, HEAD_DIM), 1.0),
        'cache_c_kv': nrm(ks[4], (DEC_BATCH, N_ODD, 2, PAST_LEN, C_KV_HEADS, HEAD_DIM), 1.0),
        'c': nrm(ks[5], (DEC_BATCH, D_MODEL), 1.0),
        'c_ctx': nrm(ks[6], (D_MODEL,), 1.0),
        'ada_w': nrm(ks[7], (DEPTH, D_MODEL, 6 * D_MODEL), 0.5 * D_MODEL ** -0.5),
        'ada_b': nrm(ks[8], (DEPTH, 6 * D_MODEL), 0.02),
        'norm1_g': 1.0 + nrm(ks[9], (DEPTH, D_MODEL), 0.02),
        'norm2_g': 1.0 + nrm(ks[10], (DEPTH, D_MODEL), 0.02),
        'final_g': 1.0 + nrm(ks[11], (D_MODEL,), 0.02),
        'mlp_w1': nrm(ks[12], (DEPTH, D_MODEL, D_FF), D_MODEL ** -0.5),
        'mlp_w2': nrm(ks[13], (DEPTH, D_FF, D_MODEL), D_FF ** -0.5),
        'ev_w_in': nrm(ks[15], (N_EVEN, D_MODEL, EVEN_IN), D_MODEL ** -0.5),
        'a_conv': nrm(ks[16], (N_EVEN, CONV_K, 3 * A_WIDTH), CONV_K ** -0.5),
        'a_log': jnp.log(jax.random.uniform(ks[17], (N_EVEN, 2, A_HEADS), F32, 1.0, 16.0)),
        'a_dt_bias': jnp.log(jnp.expm1(dt)),
        'a_norm_g': 1.0 + nrm(ks[18], (N_EVEN, HEAD_DIM), 0.02),
        'b_sink': nrm(ks[19], (N_EVEN, B_HEADS), 1.0),
        'ev_w_out': nrm(ks[20], (N_EVEN, A_WIDTH + B_WIDTH, D_MODEL), (A_WIDTH + B_WIDTH) ** -0.5),
        'od_w_in': nrm(ks[21], (N_ODD, D_MODEL, ODD_IN), D_MODEL ** -0.5),
        'c_qnorm_g': 1.0 + nrm(ks[22], (N_ODD, HEAD_DIM), 0.02),
        'c_knorm_g': 1.0 + nrm(ks[23], (N_ODD, HEAD_DIM), 0.02),
        'od_w_out': nrm(ks[24], (N_ODD, C_WIDTH, D_MODEL), C_WIDTH ** -0.5),
    }


def reference(x_prompt, x_sample, state_a, cache_b_kv, cache_c_kv, c, c_ctx, ada_w, ada_b, norm1_g, norm2_g,
              final_g, mlp_w1, mlp_w2, ev_w_in, a_conv, a_log, a_dt_bias, a_norm_g, b_sink, ev_w_out,
              od_w_in, c_qnorm_g, c_knorm_g, od_w_out):
    xp, xs = x_prompt, x_sample
    cond_p = c_ctx[None, :]
    new_a, new_b, new_c = [], [], []
    for l in range(DEPTH):
        i = l // 2
        mp = _adaln(cond_p, ada_w[l], ada_b[l])
        ms = _adaln(c, ada_w[l], ada_b[l])
        hp = _modulate(xp, norm1_g[l], mp[0], mp[1])
        hs = _modulate(xs, norm1_g[l], ms[0], ms[1])
        if l % 2 == 0:
            ev = (ev_w_in[i], a_conv[i], a_log[i], a_dt_bias[i], a_norm_g[i], b_sink[i], ev_w_out[i])
            op, sa, kvb = _even_mixer(hp, *ev, None, None)
            os_, _, _ = _even_mixer(hs, *ev, state_a[:, i], cache_b_kv[:, i])
            new_a.append(sa)
            new_b.append(kvb)
        else:
            od = (od_w_in[i], c_qnorm_g[i], c_knorm_g[i], od_w_out[i])
            op, kvc = _odd_mixer(hp, *od, None)
            os_, _ = _odd_mixer(hs, *od, cache_c_kv[:, i])
            new_c.append(kvc)
        xp = xp + mp[2] * op
        xs = xs + ms[2] * os_
        xp = xp + mp[5] * _mlp(_modulate(xp, norm2_g[l], mp[3], mp[4]), mlp_w1[l], mlp_w2[l])
        xs = xs + ms[5] * _mlp(_modulate(xs, norm2_g[l], ms[3], ms[4]), mlp_w1[l], mlp_w2[l])
    y_prompt = _rms(xp, final_g)
    y_sample = _rms(xs, final_g)
    return (y_prompt, y_sample, jnp.stack(new_a, axis=1), jnp.stack(new_b, axis=1), jnp.stack(new_c, axis=1))
```

```python
import numpy as np
import concourse.bass as bass
import concourse.mybir as mybir
from concourse.bass_utils import run_bass_kernel_spmd

F32, BF16 = mybir.dt.float32, mybir.dt.bfloat16
AF = mybir.ActivationFunctionType
ALU = mybir.AluOpType

D = 1024
T = 2048
NT = 16
DEPTH = 4
EPS = 1e-6
NEGB = -30000.0
SCALE = 128 ** -0.5
CH_DT = F32

DO_EVEN = True
DO_ODD = True
DO_GDN = True
DO_BATT = True
LAST_PROG = None
SEQ_GDN = False


class Buf:
    __slots__ = ("w", "r", "name", "excl")

    def __init__(self, name="", excl=False):
        self.w = None
        self.r = {}
        self.name = name
        self.excl = excl


class Prog:
    def __init__(self, nc, n_sp=24, n_pool=14):
        self.nc = nc
        self.E = {"pe": nc.tensor, "act": nc.scalar, "dve": nc.vector, "pool": nc.gpsimd, "sp": nc.sync}
        self.semh = {k: nc.alloc_semaphore("s_" + k) for k in self.E}
        self.cnt = {k: 0 for k in self.E}
        self.seen = {k: {} for k in self.E}
        self.dq = {"sp": [("dsp", i) for i in range(n_sp)], "pool": [("dpl", i) for i in range(n_pool)]}
        for q in self.dq:
            for key in self.dq[q]:
                self.semh[key] = nc.alloc_semaphore("%s%d" % key)
                self.cnt[key] = 0
        self.dq_next = {"sp": 0, "pool": 0}
        self.nins = 0
        self.glob = 0
        self.marks = []

    def _wait(self, eng, need):
        for k, v in need.items():
            if k == eng and eng == "pe":
                continue
            if self.seen[eng].get(k, 0) < v:
                self.E[eng].wait_ge(self.semh[k], v)
                self.glob += 1
                self.seen[eng][k] = v

    @staticmethod
    def _deps(r, w, need, eng=None):
        for b in r:
            if b.w is not None:
                k, v = b.w
                if need.get(k, 0) < v:
                    need[k] = v
            if b.excl:
                for k, v in b.r.items():
                    if k != eng and need.get(k, 0) < v:
                        need[k] = v
        for b in w:
            if b.w is not None:
                k, v = b.w
                if need.get(k, 0) < v:
                    need[k] = v
            for k, v in b.r.items():
                if need.get(k, 0) < v:
                    need[k] = v

    @staticmethod
    def _mark(r, w, tok):
        for b in w:
            b.w = tok
            b.r = {}
        for b in r:
            b.r[tok[0]] = tok[1]

    def op(self, eng, fn, r=(), w=()):
        need = {}
        self._deps(r, w, need, eng)
        self._wait(eng, need)
        ins = fn(self.E[eng])
        self.glob += 1
        self.cnt[eng] += 1
        ins.then_inc(self.semh[eng], 1)
        self._mark(r, w, (eng, self.cnt[eng]))
        self.nins += 1

    def dma(self, q, out, in_, r=(), w=(), **kw):
        i = self.dq_next[q]
        self.dq_next[q] = (i + 1) % len(self.dq[q])
        key = self.dq[q][i]
        need = {}
        self._deps(r, w, need)
        if self.cnt[key]:
            need[key] = max(need.get(key, 0), self.cnt[key])
        self._wait(q, need)
        ins = self.E[q].dma_start(out=out, in_=in_, **kw)
        self.glob += 1
        self.cnt[key] += 16
        ins.then_inc(self.semh[key], 16)
        self._mark(r, w, (key, self.cnt[key]))
        self.nins += 1

    def mark(self, name):
        self.marks.append((name, self.glob))

    def barrier(self):
        for e in ("pe", "act", "dve", "pool", "sp"):
            need = {k: v for k, v in self.cnt.items() if v > 0 and (k != e or e != "pe")}
            self._wait(e, need)


def weight_plan():
    plan = {}
    n = 0
    for l in range(DEPTH):
        plan[("ada", l)] = (n, 24); n += 24
        plan[("w1", l)] = (n, 16); n += 16
        plan[("w2", l)] = (n, 16); n += 16
        if l % 2 == 0:
            plan[("evA", l // 2)] = (n, 8); n += 8
            plan[("evB", l // 2)] = (n, 4); n += 4
            plan[("evO", l // 2)] = (n, 4); n += 4
        else:
            plan[("odI", l // 2)] = (n, 6); n += 6
            plan[("odO", l // 2)] = (n, 4); n += 4
    return plan, n


def _pack(W, kpp, cpp):
    K, N = W.shape
    kc = K // 128
    Wr = W.reshape(kc // kpp, kpp, 128, N // cpp, cpp)
    return np.ascontiguousarray(Wr.transpose(0, 3, 2, 1, 4)).reshape(-1, 128, kpp * cpp)


def prm_plan():
    m = {}
    n = 0

    def add(name, w):
        nonlocal n
        m[name] = (n, w)
        n += w

    add("cond", 8)
    for l in range(DEPTH):
        add(("ada_b", l), 48)
        add(("n1g", l), 8)
        add(("n2g", l), 8)
    add("fing", 8)
    for i in range(2):
        add(("conv", i), 36)
        add(("alog", i), 8)
        add(("dtb", i), 8)
        add(("ang", i), 1)
        add(("sink", i), 4)
        add(("cqg", i), 1)
        add(("ckg", i), 1)
        add(("ckgb", i), 128)
    return m, n


def cst_plan():
    m = {}
    n = 0

    def add(name, w):
        nonlocal n
        m[name] = (n, w)
        n += w

    for nm in ("ident", "ones", "triF", "triB", "MiF", "MsF", "MiB", "MsB", "Rm", "mPT", "mNT"):
        add(nm, 128)
    add("cos", T)
    add("sin", T)
    add("biasB", 16 * 7)
    add("biasC", 16 * 20)
    add("flag", 1)
    add("nflag", 1)
    return m, n


def build_program(debug=False):
    nc = bass.Bass("TRN2", target_bir_lowering=False)
    P = Prog(nc)
    wplan, NPIECE = weight_plan()
    pplan, NPRM = prm_plan()
    cplan, NCST = cst_plan()

    def din(name, shape, dt=F32):
        return nc.dram_tensor(name, list(shape), dt, kind="ExternalInput").ap()

    def dout(name, shape, dt=F32):
        return nc.dram_tensor(name, list(shape), dt, kind="ExternalOutput").ap()

    def dscr(name, shape, dt):
        return nc.dram_tensor(name, list(shape), dt, kind="Internal").ap()

    xT_d = din("xT", [D, T])
    wts_d = din("wts", [NPIECE, 128, 2048])
    wbg_d = din("wbg", [2, 128, 8 * 16])
    prm_d = din("prm", [128, NPRM])
    cst_d = din("cst", [128, NCST])
    s0_d = din("s0", [2, 2, 4, 128, 128])
    ckb_d = din("ckb", [2, 2, 128, 512])
    cvb_d = din("cvb", [2, 512, 256])
    ckc_d = din("ckc", [2, 2, 128, 512])
    cvc_d = din("cvc", [2, 512, 256])

    yT_d = dout("yT", [D, T])
    ost_d = dout("ost", [2, 8, 2, 4, 128, 128])
    obk_d = dout("obk", [2, 2, 128, T])
    obv_d = dout("obv", [2, T, 256])
    ock_d = dout("ock", [2, T, 256])
    ocv_d = dout("ocv", [2, T, 256])

    Aq_d = dscr("Aq", [4, 128, T], BF16)
    Ak_d = dscr("Ak", [4, 128, T], BF16)
    Av_d = dscr("Av", [4, 128, T], BF16)
    Ag_d = dscr("Ag", [4, 128, T], BF16)
    Ao_d = dscr("Ao", [4, 128, T], F32)
    Aof_d = dscr("Aof", [4, 128, T], F32)
    Z_d = dscr("Zs", [8, 128, T], BF16)
    Q_d = dscr("Qs", [8, 128, T], BF16)
    K_d = dscr("Ks", [2, 128, T], BF16)
    V_d = dscr("Vs", [T, 256], BF16)
    scrB = {k: Buf(k) for k in ("Aq", "Ak", "Av", "Ag", "Ao", "Z", "Q", "K", "V")}

    from contextlib import ExitStack, contextmanager
    stacks = [ExitStack()]
    uniq = {"n": 0}

    def sb(name, shape, dt):
        uniq["n"] += 1
        return stacks[-1].enter_context(nc.sbuf_tensor("%s_%d" % (name, uniq["n"]), list(shape), dt))

    @contextmanager
    def phase():
        st = ExitStack()
        stacks.append(st)
        try:
            yield
        finally:
            P.barrier()
            stacks.pop()
            st.close()

    XT = sb("XT", [128, 8, T], F32)
    XB = [[Buf("XT%d_%d" % (dc, tc)) for tc in range(4)] for dc in range(8)]
    CST = sb("CST", [128, NCST], F32)
    PRM = sb("PRM", [128, NPRM], F32)
    cB = Buf("cst")
    NSLOT = 5
    ring = {"WS": None, "WSB": None, "i": 0}
    ring["WS"] = [sb("WS%d" % i, [128, 2048], BF16) for i in range(NSLOT)]
    ring["WSB"] = [Buf("WS%d" % i) for i in range(NSLOT)]

    def ring_alloc():
        pass
    identb = sb("identb", [128, 128], BF16)
    onesb = sb("onesb", [128, 128], BF16)
    MOD = sb("MOD", [128, DEPTH, 48], F32)
    A1 = sb("A1", [128, DEPTH, 8], F32)
    A2 = sb("A2", [128, DEPTH, 8], F32)
    modB = [Buf("mod%d" % l) for l in range(DEPTH)]
    scb = sb("scb", [128, 8], BF16)
    zero8 = sb("zero8", [128, 8], F32)
    WBG = sb("WBG", [128, 2, 8 * 16], BF16)
    BETA = sb("BETA", [128, NT, 8], F32)
    GG = sb("GG", [128, NT, 8], F32)
    bgB = Buf("bg")
    EXS = sb("EXS", [128, 8], F32)
    NEA = sb("NEA", [128, 16], F32)

    PSB = [nc.alloc_psum_tensor("ps%d" % i, [128, 512], F32) for i in range(8)]
    PB = [Buf("ps%d" % i, excl=True) for i in range(8)]

    def cc(name, lo=0, hi=None):
        o, w = cplan[name]
        hi = w if hi is None else hi
        return CST[:, o + lo:o + hi]

    def pc(name, lo=0, hi=None):
        o, w = pplan[name]
        hi = w if hi is None else hi
        return PRM[:, o + lo:o + hi]

    def mm(out, lhsT, rhs, start, stop, r, w):
        P.op("pe", lambda e: e.matmul(out, lhsT=lhsT, rhs=rhs, start=start, stop=stop), r, w)

    def tr(out, in_, ident, r, w):
        P.op("pe", lambda e: e.transpose(out=out, in_=in_, identity=ident), r, w)

    def act(out, in_, func, r, w, **kw):
        P.op("act", lambda e: e.activation(out=out, in_=in_, func=func, **kw), r, w)

    def tt(eng, out, in0, in1, op, r, w):
        P.op(eng, lambda e: e.tensor_tensor(out=out, in0=in0, in1=in1, op=op), r, w)

    def ts(eng, out, in0, s1, op0, r, w, s2=None, op1=None):
        if op1 is None:
            P.op(eng, lambda e: e.tensor_scalar(out=out, in0=in0, scalar1=s1, scalar2=None, op0=op0), r, w)
        else:
            P.op(eng, lambda e: e.tensor_scalar(out=out, in0=in0, scalar1=s1, scalar2=s2, op0=op0, op1=op1), r, w)

    def stt(out, in0, scalar, in1, op0, op1, r, w):
        P.op("dve", lambda e: e.scalar_tensor_tensor(out=out, in0=in0, scalar=scalar, in1=in1, op0=op0, op1=op1), r, w)

    def cp(eng, out, in_, r, w):
        P.op(eng, lambda e: e.tensor_copy(out=out, in_=in_), r, w)

    def recip(out, in_, r, w):
        act(out, in_, AF.Ln, r, w)
        act(out, out, AF.Exp, list(w), w, scale=-1.0)

    def rsqrt(out, in_, scale, eps, tmp, r, w, tmpB):
        act(tmp, in_, AF.Ln, list(r) + [cB], [tmpB], scale=scale, bias=epsc[:, 0:1])
        act(out, tmp, AF.Exp, [tmpB], w, scale=-0.5)

    def wload(piece):
        k = ring["i"]
        ring["i"] = (k + 1) % NSLOT
        WS, WSB = ring["WS"], ring["WSB"]
        P.dma("pool", out=WS[k][:], in_=wts_d[piece], r=[], w=[WSB[k]])
        return WS[k], WSB[k]

    pstate = {}

    def psum(banks=(0, 1, 2, 3, 4, 5, 6)):
        i = pstate.get(banks, 0)
        pstate[banks] = i + 1
        b = banks[i % len(banks)]
        return PSB[b], PB[b]

    def run_window(gen_iter, width=2):
        gen_iter = iter(gen_iter)
        active = []
        done = False
        while True:
            while not done and len(active) < width:
                try:
                    active.append(next(gen_iter))
                except StopIteration:
                    done = True
            if not active:
                break
            for g in list(active):
                try:
                    next(g)
                except StopIteration:
                    active.remove(g)

    def bcl(ap2, n):
        return ap2.unsqueeze(2).to_broadcast([128, ap2.shape[1], n])

    def bcm(ap2, k):
        return ap2.unsqueeze(1).to_broadcast([128, k, ap2.shape[1]])

    def v4(ap):
        return ap.rearrange("p (a b) -> p a b", b=128)

    P.dma("sp", out=CST[:], in_=cst_d[:, :], w=[cB])
    P.dma("sp", out=PRM[:], in_=prm_d[:, :], w=[cB])
    for dc in range(8):
        P.dma("sp", out=XT[:, dc, :], in_=xT_d[dc * 128:(dc + 1) * 128, :], w=XB[dc])
    P.dma("pool", out=WBG[:], in_=wbg_d.rearrange("i p c -> p i c"), w=[cB])
    epsc = sb("epsc", [128, 1], F32)
    P.op("dve", lambda e: e.memset(epsc[:], EPS), [], [cB])
    P.op("dve", lambda e: e.memset(zero8[:], 0.0), [], [cB])
    cp("dve", identb[:], cc("ident"), [cB], [cB])
    cp("dve", onesb[:], cc("ones"), [cB], [cB])
    act(scb[:], pc("cond"), AF.Silu, [cB], [cB])
    for i in range(2):
        act(EXS[:, i * 4:(i + 1) * 4], pc(("sink", i)), AF.Exp, [cB], [cB])
        act(NEA[:, i * 8:(i + 1) * 8], pc(("alog", i)), AF.Exp, [cB], [cB])
    ts("dve", NEA[:], NEA[:], -1.0, ALU.mult, [cB], [cB])

    def adaln_piece(l, pi):
        p0, npc = wplan[("ada", l)]
        pt, pb = PSB[7], PB[7]
        wt, wb = wload(p0 + pi)
        wv = wt[:].rearrange("p (k c) -> p k c", k=8)
        for ft in range(2):
            j = pi * 2 + ft
            for kc in range(8):
                mm(pt[:, j:j + 1], wv[:, kc, ft * 128:(ft + 1) * 128], scb[:, kc:kc + 1], kc == 0, kc == 7,
                   [wb, cB], [pb])

    def adaln_finish(l):
        pt, pb = PSB[7], PB[7]
        tt("dve", MOD[:, l, :], pt[:, 0:48], pc(("ada_b", l)), ALU.add, [pb, cB], [modB[l]])
        ts("dve", A1[:, l, :], MOD[:, l, 8:16], 1.0, ALU.add, [modB[l]], [modB[l]])
        tt("dve", A1[:, l, :], A1[:, l, :], pc(("n1g", l)), ALU.mult, [modB[l], cB], [modB[l]])
        ts("dve", A2[:, l, :], MOD[:, l, 32:40], 1.0, ALU.add, [modB[l]], [modB[l]])
        tt("dve", A2[:, l, :], A2[:, l, :], pc(("n2g", l)), ALU.mult, [modB[l], cB], [modB[l]])

    def adaln(l):
        for pi in range(24):
            adaln_piece(l, pi)
        adaln_finish(l)

    def norm_mod(A, SH, dstf, dstB, lB, loc, post=None, dst_all=None):
        def chain(tc):
            SQ, sqB, RS, rsB, TM, tmB, R0, r0B = loc[tc % 2]
            cs = slice(tc * 512, (tc + 1) * 512)
            act(SQ[:], XT[:, :, cs], AF.Square, [XB[dc][tc] for dc in range(8)], [sqB])
            yield
            pt, pb = psum((4, 5, 6))
            for dc in range(8):
                mm(pt[:], onesb[:], SQ[:, dc, :], dc == 0, dc == 7, [sqB, cB], [pb])
            yield
            act(R0[:], pt[:], AF.Ln, [pb, cB], [r0B], scale=1.0 / D, bias=epsc[:, 0:1])
            yield
            act(RS[:], R0[:], AF.Exp, [r0B], [rsB], scale=-0.5)
            yield
            for dc in range(8):
                stt(TM[:, dc, :], XT[:, dc, cs], A[:, dc:dc + 1], RS[:], ALU.mult, ALU.mult,
                    [XB[dc][tc], rsB, lB, cB], [tmB[dc]])
                if dc % 4 == 3:
                    yield
            tt("pool", dst_all(tc), TM[:], bcl(SH, 512), ALU.add, list(tmB) + [lB, cB],
               [dstB(dc, tc) for dc in range(8)])
            if post is not None:
                for dc in range(8):
                    post(dc, tc)
            yield

        run_window((chain(tc) for tc in range(4)), 2)

    def norm_locals():
        out = []
        for _ in range(2):
            SQ = sb("SQ", [128, 8, 512], BF16)
            RS = sb("RS", [128, 512], F32)
            TM = sb("TM", [128, 8, 512], F32)
            R0 = sb("R0", [128, 512], F32)
            out.append((SQ, Buf("sq"), RS, Buf("rs"), TM, [Buf("tm%d" % i) for i in range(8)], R0, Buf("r0")))
        return out

    def mlp(l, HT, HB):
        H1 = [sb("H1a", [128, 4, 512], BF16), sb("H1b", [128, 4, 512], BF16)]
        H1B = [[Buf("h1") for _ in range(4)] for _ in range(2)]
        RL = [sb("RLa", [128, 512], BF16), sb("RLb", [128, 512], BF16)]
        RLB = [Buf("rl"), Buf("rl")]
        p1, _ = wplan[("w1", l)]
        p2, _ = wplan[("w2", l)]
        wcache = {}
        cnt = {"rl": 0}

        def weights(g):
            if g not in wcache:
                w1s = [wload(p1 + g * 2 + i) for i in range(2)]
                w2s = [wload(p2 + g * 2 + i) for i in range(2)]
                wcache[g] = (w1s, w2s)
            return wcache[g]

        def ph1(k):
            g, tt_ = k // 4, k % 4
            w1s, _ = weights(g)
            cs = slice(tt_ * 512, (tt_ + 1) * 512)
            hb = k % 2
            for ft in range(4):
                wt, wb = w1s[ft // 2]
                wv = wt[:].rearrange("p (k c) -> p k c", k=8)
                pt, pb = psum((0, 1, 2, 3))
                for kc in range(8):
                    mm(pt[:], wv[:, kc, (ft % 2) * 128:(ft % 2 + 1) * 128], HT[:, kc, cs], kc == 0, kc == 7,
                       [wb] + HB(kc, tt_), [pb])
                rb = cnt["rl"] % 2
                cnt["rl"] += 1
                act(RL[rb][:], pt[:], AF.Relu, [pb], [RLB[rb]])
                tt("pool", H1[hb][:, ft, :], RL[rb][:], RL[rb][:], ALU.mult, [RLB[rb]], [H1B[hb][ft]])

        def ph2(k):
            g, tt_ = k // 4, k % 4
            _, w2s = weights(g)
            cs = slice(tt_ * 512, (tt_ + 1) * 512)
            hb = k % 2
            for dt in range(8):
                wt, wb = w2s[dt // 4]
                wv = wt[:].rearrange("p (k c) -> p k c", k=4)
                pt, pb = psum((4, 5, 6))
                for fc in range(4):
                    mm(pt[:], wv[:, fc, (dt % 4) * 128:(dt % 4 + 1) * 128], H1[hb][:, fc, :], fc == 0, fc == 3,
                       [wb, H1B[hb][fc]], [pb])
                stt(XT[:, dt, cs], pt[:], MOD[:, l, 40 + dt:41 + dt], XT[:, dt, cs], ALU.mult, ALU.add,
                    [pb, modB[l], XB[dt][tt_]], [XB[dt][tt_]])

        ph1(0)
        for k in range(32):
            last_in_group = k % 4 == 3
            if not last_in_group:
                ph1(k + 1)
            ph2(k)
            if last_in_group:
                g = k // 4
                if l + 1 < DEPTH:
                    for pi in range(3 * g, 3 * g + 3):
                        adaln_piece(l + 1, pi)
                if k + 1 < 32:
                    ph1(k + 1)
        if l + 1 < DEPTH:
            adaln_finish(l + 1)

    def out_proj(l, key, Z, ZB):
        p0, _ = wplan[key]
        for pi in range(4):
            wt, wb = wload(p0 + pi)
            wv = wt[:].rearrange("p (k c) -> p k c", k=8)
            for d2 in range(2):
                dt = pi * 2 + d2
                for tc in range(4):
                    cs = slice(tc * 512, (tc + 1) * 512)
                    pt, pb = psum((0, 1, 2, 3))
                    for zc in range(8):
                        mm(pt[:], wv[:, zc, d2 * 128:(d2 + 1) * 128], Z[:, zc, cs], zc == 0, zc == 7,
                           [wb] + ZB(zc, tc), [pb])
                    stt(XT[:, dt, cs], pt[:], MOD[:, l, 16 + dt:17 + dt], XT[:, dt, cs], ALU.mult, ALU.add,
                        [pb, modB[l], XB[dt][tc]], [XB[dt][tc]])

    def rope_to(dst, src, tc, loc, rB, wB):
        T1, t1B, T2, t2B = loc
        pt, pb = psum((0, 1, 2, 3))
        mm(pt[:], cc("Rm"), src, True, True, list(rB) + [cB], [pb])
        tt("pool", T1[:], src, cc("cos", tc * 512, (tc + 1) * 512), ALU.mult, list(rB) + [cB], [t1B])
        tt("dve", T2[:], pt[:], cc("sin", tc * 512, (tc + 1) * 512), ALU.mult, [pb, cB], [t2B])
        tt("dve", dst, T1[:], T2[:], ALU.add, [t1B, t2B], wB)

    def small_locals(nsets=2, rope=True):
        sets = []
        for _ in range(nsets):
            d = {}
            d["SQh"] = sb("SQh", [128, 512], BF16); d["sqhB"] = Buf()
            d["RSh"] = sb("RSh", [128, 512], F32); d["rshB"] = Buf()
            d["R0h"] = sb("R0h", [128, 512], F32); d["r0hB"] = Buf()
            if rope:
                d["QN"] = sb("QN", [128, 512], F32); d["qnB"] = Buf()
                d["rope"] = (sb("T1", [128, 512], F32), Buf(), sb("T2", [128, 512], F32), Buf())
            sets.append(d)
        return sets

    def fm_proj(wv, wb, ft, HT, HB, tc, dstfn):
        cs = slice(tc * 512, (tc + 1) * 512)
        pt, pb = psum((0, 1, 2, 3))
        for kc in range(8):
            mm(pt[:], wv[:, kc, ft * 128:(ft + 1) * 128], HT[:, kc, cs], kc == 0, kc == 7, [wb] + HB(kc, tc), [pb])
        dstfn(pt, pb)

    def odd_inproj(l, i, HT, HB):
        p0, _ = wplan[("odI", i)]
        sls = small_locals()
        RAW = sb("RAW", [128, 2, T], F32)
        rawB = [[Buf() for _ in range(4)] for _ in range(2)]
        QF = sb("QF", [128, T], BF16)
        qfB = [Buf() for _ in range(4)]
        QF2 = [QF, sb("QF2", [128, T], BF16)]
        qfB2 = [qfB, [Buf() for _ in range(4)]]
        wq = {}

        def qproj(h):
            pi, ft = h // 2, h % 2
            if ft == 0:
                wt, wb = wload(p0 + pi)
                wq[pi] = (wt[:].rearrange("p (k c) -> p k c", k=8), wb)
            wv, wb = wq[pi]
            sl_ = h % 2
            for tc in range(4):
                cs = slice(tc * 512, (tc + 1) * 512)
                fm_proj(wv, wb, ft, HT, HB, tc,
                        lambda pt, pb, tc=tc, cs=cs: act(RAW[:, sl_, cs], pt[:], AF.Copy, [pb], [rawB[sl_][tc]]))

        def qchain(h, tc):
            sl_ = h % 2
            sl = sls[tc % 2]
            cs = slice(tc * 512, (tc + 1) * 512)
            act(sl["SQh"][:], RAW[:, sl_, cs], AF.Square, [rawB[sl_][tc]], [sl["sqhB"]])
            yield
            pt, pb = psum((4, 5, 6))
            mm(pt[:], onesb[:], sl["SQh"][:], True, True, [sl["sqhB"], cB], [pb])
            yield
            act(sl["R0h"][:], pt[:], AF.Ln, [pb, cB], [sl["r0hB"]], scale=1.0 / 128, bias=epsc[:, 0:1])
            yield
            act(sl["RSh"][:], sl["R0h"][:], AF.Exp, [sl["r0hB"]], [sl["rshB"]], scale=-0.5)
            yield
            stt(sl["QN"][:], RAW[:, sl_, cs], pc(("cqg", i)), sl["RSh"][:], ALU.mult, ALU.mult,
                [rawB[sl_][tc], sl["rshB"], cB], [sl["qnB"]])
            yield
            T1, t1B, T2, t2B = sl["rope"]
            pt2, pb2 = psum((0, 1, 2, 3))
            mm(pt2[:], cc("Rm"), sl["QN"][:], True, True, [sl["qnB"], cB], [pb2])
            tt("pool", T1[:], sl["QN"][:], cc("cos", tc * 512, (tc + 1) * 512), ALU.mult, [sl["qnB"], cB], [t1B])
            yield
            tt("dve", T2[:], pt2[:], cc("sin", tc * 512, (tc + 1) * 512), ALU.mult, [pb2, cB], [t2B])
            yield
            tt("dve", QF2[sl_][:, cs], T1[:], T2[:], ALU.add, [t1B, t2B], [qfB2[sl_][tc]])
            yield

        def qpost(h):
            sl_ = h % 2
            run_window((qchain(h, tc) for tc in range(4)), 2)
            P.dma("sp", out=Q_d[h], in_=QF2[sl_][:], r=qfB2[sl_], w=[Buf()])

        qproj(0)
        for h in range(8):
            if h + 1 < 8:
                qproj(h + 1)
            qpost(h)
        wk, wkb = wload(p0 + 4)
        wvv, wvb = wload(p0 + 5)
        wkv = wk[:].rearrange("p (k c) -> p k c", k=8)
        wvv_ = wvv[:].rearrange("p (k c) -> p k c", k=8)
        KN = [sb("KN", [128, 2, 128], F32) for _ in range(2)]
        knB = [Buf(), Buf()]
        VF = [sb("VF", [128, 256], F32) for _ in range(2)]
        vfB = [Buf(), Buf()]
        JK = [sb("JK", [128, 128], BF16) for _ in range(2)]
        jkB = [Buf(), Buf()]
        SSK = [sb("SSK", [128, 2], F32) for _ in range(2)]
        SSK2 = [sb("SSK2", [128, 2], F32) for _ in range(2)]
        RK = [sb("RK", [128, 2], F32) for _ in range(2)]
        skB = [Buf(), Buf()]

        def kv_tile(n):
            b = n % 2
            rs = slice(n * 128, (n + 1) * 128)
            pt, pb = psum((0, 1, 2, 3))
            for kc in range(8):
                mm(pt[:, 0:256], HT[:, kc, rs], wkv[:, kc, :], kc == 0, kc == 7, [wkb] + HB(kc, n // 4), [pb])
            for kc in range(8):
                mm(pt[:, 256:512], HT[:, kc, rs], wvv_[:, kc, :], kc == 0, kc == 7, [wvb] + HB(kc, n // 4), [pb])
            yield
            for kv in range(2):
                act(JK[b][:], pt[:, kv * 128:(kv + 1) * 128], AF.Square, [pb], [jkB[b], skB[b]],
                    accum_out=SSK[b][:, kv:kv + 1])
            act(VF[b][:], pt[:, 256:512], AF.Copy, [pb], [vfB[b]])
            yield
            ts("dve", SSK2[b][:], SSK[b][:], 1.0 / 128, ALU.mult, [skB[b]], [skB[b]], s2=EPS, op1=ALU.add)
            P.dma("sp", out=ocv_d[i, rs, :], in_=VF[b][:], r=[vfB[b]], w=[Buf()])
            P.dma("pool", out=V_d[rs, :], in_=VF[b][:], r=[vfB[b]], w=[Buf()])
            yield
            act(SSK2[b][:], SSK2[b][:], AF.Ln, [skB[b]], [skB[b]])
            yield
            act(RK[b][:], SSK2[b][:], AF.Exp, [skB[b]], [skB[b]], scale=-0.5)
            yield
            for kv in range(2):
                stt(KN[b][:, kv, :], pt[:, kv * 128:(kv + 1) * 128], RK[b][:, kv:kv + 1], pc(("ckgb", i)),
                    ALU.mult, ALU.mult, [pb, skB[b], cB], [knB[b]])
            yield
            P.dma("sp", out=ock_d[i, rs, :], in_=KN[b][:].rearrange("p a b -> p (a b)"), r=[knB[b]], w=[Buf()])
            pt2, pb2 = psum((4, 5, 6))
            for kv in range(2):
                tr(pt2[:, kv * 128:(kv + 1) * 128], KN[b][:, kv, :], cc("ident"), [knB[b], cB], [pb2])
            yield
            act(RAW[:, :, rs], pt2[:, 0:256].rearrange("p (a b) -> p a b", a=2), AF.Copy, [pb2],
                [rawB[0][n // 4], rawB[1][n // 4]])
            yield

        run_window((kv_tile(n) for n in range(NT)), 2)
        for kv in range(2):
            for tc in range(4):
                cs = slice(tc * 512, (tc + 1) * 512)
                rope_to(QF2[kv][:, cs], RAW[:, kv, cs], tc, sls[tc % 2]["rope"], [rawB[kv][tc]], [qfB2[kv][tc]])
            P.dma("sp", out=K_d[kv], in_=QF2[kv][:], r=qfB2[kv], w=[Buf()])

    def attn_loads(ck_d, cv_d, i, nq):
        QZ = sb("QZ", [128, nq, T], BF16)
        qzB = [[Buf() for _ in range(NT)] for _ in range(2)]
        KT = sb("KT", [128, 2, T], BF16); ktB = Buf()
        V = sb("V", [128, NT, 256], BF16); vB = Buf()
        CK = sb("CK", [128, 2, 512], BF16)
        CV = sb("CV", [128, 4, 256], BF16)
        g = nq // 2
        for h in range(nq):
            P.dma("sp", out=QZ[:, h, :], in_=Q_d[h], w=qzB[h // g])
        for kv in range(2):
            P.dma("sp", out=KT[:, kv, :], in_=K_d[kv], w=[ktB])
        P.dma("sp", out=V[:], in_=V_d.rearrange("(n p) c -> p n c", p=128), w=[vB])
        P.dma("pool", out=CK[:], in_=ck_d[i].rearrange("k d t -> d k t"), w=[ktB])
        P.dma("pool", out=CV[:], in_=cv_d[i].rearrange("(m p) c -> p m c", p=128), w=[vB])
        return QZ, qzB, KT, ktB, V, vB, CK, CV

    def odd_attn(l, i):
        QZ, qzB, KT, ktB, V, vB, CK, CV = attn_loads(ckc_d, cvc_d, i, 8)
        NPT = 6
        PT = [sb("PT", [128, 512], BF16) for _ in range(NPT)]
        ptB = [Buf() for _ in range(NPT)]
        RC = [sb("RC", [128, 512], F32) for _ in range(2)]
        rcB = [Buf(), Buf()]
        cnt = {"it": 0}

        def unit(kv, n):
            ns = slice(n * 128, (n + 1) * 128)
            rhs = QZ[:, kv * 4:(kv + 1) * 4, ns]
            ot, otb = psum((4, 5))
            su, sub = psum((6, 7))

            def qk(m):
                st, stb = psum((0, 1, 2, 3))
                lhsT = CK[:, kv, m * 128:(m + 1) * 128] if m < 4 else KT[:, kv, (m - 4) * 128:(m - 3) * 128]
                mm(v4(st[:]), lhsT, rhs, True, True, [ktB, qzB[kv][n]], [stb])
                return st, stb

            nxt = qk(0)
            yield
            for m in range(20):
                st, stb = nxt
                if m + 1 < 20:
                    nxt = qk(m + 1)
                pi = cnt["it"] % NPT
                cnt["it"] += 1
                act(PT[pi][:], st[:], AF.Exp, [stb, cB], [ptB[pi]], scale=SCALE,
                    bias=cc("biasC", n * 20 + m, n * 20 + m + 1))
                vl = CV[:, m, kv * 128:(kv + 1) * 128] if m < 4 else V[:, m - 4, kv * 128:(kv + 1) * 128]
                mm(ot[:], vl, PT[pi][:], m == 0, m == 19, [ptB[pi], vB], [otb])
                mm(su[:], onesb[:], PT[pi][:], m == 0, m == 19, [ptB[pi], cB], [sub])
                yield
            rb = (kv * NT + n) % 2
            recip(RC[rb][:], su[:], [sub], [rcB[rb]])
            tt("dve", QZ[:, kv * 4:(kv + 1) * 4, ns], v4(ot[:]), v4(RC[rb][:]), ALU.mult,
               [otb, rcB[rb]], [qzB[kv][n]])
            yield

        run_window((unit(kv, n) for kv in range(2) for n in range(NT)), 2)
        out_proj(l, ("odO", i), QZ, lambda zc, tc: [qzB[zc // 4][n] for n in range(tc * 4, tc * 4 + 4)])

    def even_inprojA(l, i, HT, HB):
        p0, _ = wplan[("evA", i)]
        sls = small_locals(2, rope=False)
        RAWs = [sb("RAWc", [128, T + 2], F32) for _ in range(2)]
        rawBs = [Buf(), Buf()]
        CVs = [sb("CVt", [128, T], F32) for _ in range(2)]
        cvBs = [[Buf() for _ in range(4)] for _ in range(2)]
        FINs = [sb("FIN", [128, T], BF16) for _ in range(2)]
        finBs = [[Buf() for _ in range(4)] for _ in range(2)]
        NC0 = sb("NC0", [128, 2, 2], F32); ncB = [Buf(), Buf()]
        for k in range(2):
            P.op("dve", lambda e, k=k: e.memset(RAWs[k][:, 0:1], 0.0), [], [rawBs[k]])
            P.op("dve", lambda e, k=k: e.memset(RAWs[k][:, T + 1:T + 2], 0.0), [], [rawBs[k]])
        dsts = [Aq_d, Ak_d, Av_d, Ag_d]
        wcache = {}

        def proj(j):
            pi, ft = j // 2, j % 2
            kind = pi // 2
            if ft == 0:
                wt, wb = wload(p0 + pi)
                wcache[pi] = (wt[:].rearrange("p (k c) -> p k c", k=8), wb)
            wv, wb = wcache[pi]
            k = j % 2
            for tc in range(4):
                cs = slice(tc * 512, (tc + 1) * 512)
                if kind == 3:
                    fm_proj(wv, wb, ft, HT, HB, tc,
                            lambda pt, pb, tc=tc, cs=cs: act(FINs[k][:, cs], pt[:], AF.Silu, [pb], [finBs[k][tc]]))
                else:
                    fm_proj(wv, wb, ft, HT, HB, tc,
                            lambda pt, pb, tc=tc: cp("dve", RAWs[k][:, 1 + tc * 512:1 + (tc + 1) * 512], pt[:],
                                                     [pb], [rawBs[k]]))

        def post(j):
            pi, ft = j // 2, j % 2
            kind = pi // 2
            h = (pi % 2) * 2 + ft
            k = j % 2
            RAW, rawB, CVt, cvB, FIN, finB = RAWs[k], rawBs[k], CVs[k], cvBs[k], FINs[k], finBs[k]
            if kind == 3:
                P.dma("sp", out=Ag_d[h], in_=FIN[:], r=finB, w=[Buf()])
                return
            ct = kind * 4 + h
            w0 = pc(("conv", i), ct * 3 + 0, ct * 3 + 1)
            w1 = pc(("conv", i), ct * 3 + 1, ct * 3 + 2)
            w2 = pc(("conv", i), ct * 3 + 2, ct * 3 + 3)
            ts("dve", CVt[:], RAW[:, 1:T + 1], w1, ALU.mult, [rawB, cB], cvB)
            stt(CVt[:], RAW[:, 0:T], w0, CVt[:], ALU.mult, ALU.add, [rawB, cB] + cvB, cvB)
            stt(CVt[:], RAW[:, 2:T + 2], w2, CVt[:], ALU.mult, ALU.add, [rawB, cB] + cvB, cvB)
            ts("dve", NC0[:, k, 0:1], w0, cc("nflag"), ALU.mult, [cB], [ncB[k]], s2=-1.0, op1=ALU.mult)
            ts("dve", NC0[:, k, 1:2], w2, cc("nflag"), ALU.mult, [cB], [ncB[k]], s2=-1.0, op1=ALU.mult)
            stt(CVt[:, 256:T:256], RAW[:, 256:T:256], NC0[:, k, 0:1], CVt[:, 256:T:256], ALU.mult, ALU.add,
                [rawB, ncB[k]] + cvB, cvB)
            stt(CVt[:, 255:T - 1:256], RAW[:, 257:T + 1:256], NC0[:, k, 1:2], CVt[:, 255:T - 1:256], ALU.mult, ALU.add,
                [rawB, ncB[k]] + cvB, cvB)
            if kind == 2:
                act(FIN[:], CVt[:], AF.Silu, cvB, finB)
            else:
                def chain(tc):
                    sl = sls[tc % 2]
                    cs = slice(tc * 512, (tc + 1) * 512)
                    act(CVt[:, cs], CVt[:, cs], AF.Silu, [cvB[tc]], [cvB[tc]])
                    yield
                    act(sl["SQh"][:], CVt[:, cs], AF.Square, [cvB[tc]], [sl["sqhB"]])
                    yield
                    pt, pb = psum((4, 5, 6))
                    mm(pt[:], onesb[:], sl["SQh"][:], True, True, [sl["sqhB"], cB], [pb])
                    yield
                    act(sl["R0h"][:], pt[:], AF.Ln, [pb, cB], [sl["r0hB"]], scale=1.0, bias=epsc[:, 0:1])
                    yield
                    act(sl["RSh"][:], sl["R0h"][:], AF.Exp, [sl["r0hB"]], [sl["rshB"]], scale=-0.5)
                    yield
                    stt(FIN[:, cs], CVt[:, cs], SCALE if kind == 0 else 1.0, sl["RSh"][:], ALU.mult, ALU.mult,
                        [cvB[tc], sl["rshB"]], [finB[tc]])
                    yield

                run_window((chain(tc) for tc in range(4)), 2)
            P.dma("sp", out=dsts[kind][h], in_=FIN[:], r=finB, w=[Buf()])

        proj(0)
        for j in range(16):
            if j + 1 < 16:
                proj(j + 1)
            post(j)
        BGR = sb("BGR", [128, NT, 16], F32); bgrB = Buf()
        E1 = sb("bE1", [128, NT, 8], F32); XA = sb("bXA", [128, NT, 8], F32)
        AB = sb("bAB", [128, NT, 8], F32); L2 = sb("bL2", [128, NT, 8], F32)
        for n in range(NT):
            rs = slice(n * 128, (n + 1) * 128)
            pt, pb = psum((4, 5, 6))
            for kc in range(8):
                mm(pt[:, 0:16], HT[:, kc, rs], WBG[:, i, kc * 16:(kc + 1) * 16], kc == 0, kc == 7,
                   [cB] + HB(kc, n // 4), [pb])
            cp("dve", BGR[:, n, :], pt[:, 0:16], [pb], [bgrB])
        act(E1[:], BGR[:, :, 0:8], AF.Exp, [bgrB], [bgrB], scale=-1.0)
        ts("dve", E1[:], E1[:], 1.0, ALU.add, [bgrB], [bgrB])
        recip(BETA[:], E1[:], [bgrB], [bgB])
        tt("dve", XA[:], BGR[:, :, 8:16], bcm(pc(("dtb", i)), NT), ALU.add, [bgrB, cB], [bgrB])
        stt(AB[:], XA[:], -1.0, XA[:], ALU.mult, ALU.max, [bgrB], [bgrB])
        act(AB[:], AB[:], AF.Exp, [bgrB], [bgrB], scale=-1.0)
        act(L2[:], AB[:], AF.Ln, [bgrB], [bgrB], bias=1.0)
        stt(XA[:], XA[:], 0.0, L2[:], ALU.max, ALU.add, [bgrB], [bgrB])
        tt("dve", GG[:], XA[:], bcm(NEA[:, i * 8:(i + 1) * 8], NT), ALU.mult, [bgrB, cB], [bgB])

    def even_inprojB(l, i, HT, HB):
        p0, _ = wplan[("evB", i)]
        sls = small_locals()
        RAWs = [sb("RAWb", [128, T], F32) for _ in range(2)]
        rawBs = [[Buf() for _ in range(4)] for _ in range(2)]
        QFs = [sb("QFb", [128, T], BF16) for _ in range(2)]
        qfBs = [[Buf() for _ in range(4)] for _ in range(2)]
        wcache = {}

        def proj(j):
            pi, ft = j // 2, j % 2
            if ft == 0:
                wt, wb = wload(p0 + pi)
                wcache[pi] = (wt[:].rearrange("p (k c) -> p k c", k=8), wb)
            wv, wb = wcache[pi]
            RAW, rawB = RAWs[j % 2], rawBs[j % 2]
            for tc in range(4):
                cs = slice(tc * 512, (tc + 1) * 512)
                fm_proj(wv, wb, ft, HT, HB, tc,
                        lambda pt, pb, tc=tc, cs=cs: act(RAW[:, cs], pt[:], AF.Copy, [pb], [rawB[tc]]))

        def post(j):
            pi, ft = j // 2, j % 2
            RAW, rawB = RAWs[j % 2], rawBs[j % 2]
            QF, qfB = QFs[j % 2], qfBs[j % 2]
            if pi == 2:
                P.dma("sp", out=obk_d[i, ft], in_=RAW[:], r=rawB, w=[Buf()])
            def rchain(tc):
                cs = slice(tc * 512, (tc + 1) * 512)
                T1, t1B, T2, t2B = sls[tc % 2]["rope"]
                pt, pb = psum((0, 1, 2, 3))
                mm(pt[:], cc("Rm"), RAW[:, cs], True, True, [rawB[tc], cB], [pb])
                tt("pool", T1[:], RAW[:, cs], cc("cos", tc * 512, (tc + 1) * 512), ALU.mult, [rawB[tc], cB], [t1B])
                yield
                tt("dve", T2[:], pt[:], cc("sin", tc * 512, (tc + 1) * 512), ALU.mult, [pb, cB], [t2B])
                yield
                tt("dve", QF[:, cs], T1[:], T2[:], ALU.add, [t1B, t2B], [qfB[tc]])
                yield

            run_window((rchain(tc) for tc in range(4)), 2)
            if pi < 2:
                P.dma("sp", out=Q_d[pi * 2 + ft], in_=QF[:], r=qfB, w=[Buf()])
            else:
                P.dma("sp", out=K_d[ft], in_=QF[:], r=qfB, w=[Buf()])

        proj(0)
        for j in range(6):
            if j + 1 < 6:
                proj(j + 1)
            post(j)
        wt, wb = wload(p0 + 3)
        wv = wt[:].rearrange("p (k c) -> p k c", k=8)
        VF = [sb("VFb", [128, 256], F32) for _ in range(2)]
        vfB = [Buf(), Buf()]
        for n in range(NT):
            b = n % 2
            rs = slice(n * 128, (n + 1) * 128)
            pt, pb = psum((4, 5, 6))
            for kc in range(8):
                mm(pt[:, 0:256], HT[:, kc, rs], wv[:, kc, :], kc == 0, kc == 7, [wb] + HB(kc, n // 4), [pb])
            act(VF[b][:], pt[:, 0:256], AF.Copy, [pb], [vfB[b]])
            P.dma("sp", out=obv_d[i, rs, :], in_=VF[b][:], r=[vfB[b]], w=[Buf()])
            P.dma("pool", out=V_d[rs, :], in_=VF[b][:], r=[vfB[b]], w=[Buf()])

    def even_attnB(l, i):
        QZ, qzB, KT, ktB, V, vB, CK, CV = attn_loads(ckb_d, cvb_d, i, 4)
        PT = [sb("PTb", [128, 256], BF16) for _ in range(6)]
        ptB = [Buf() for _ in range(6)]
        DN = [sb("DN", [128, 256], F32) for _ in range(2)]
        RC = [sb("RCb", [128, 256], F32) for _ in range(2)]
        rcB = [Buf(), Buf()]
        MPT = sb("MPT", [128, 128], BF16)
        MNT = sb("MNT", [128, 128], BF16)
        mB = Buf()
        cp("dve", MPT[:], cc("mPT"), [cB], [mB])
        cp("dve", MNT[:], cc("mNT"), [cB], [mB])
        cnt = {"it": 0}

        def unit(kv, n):
            ns = slice(n * 128, (n + 1) * 128)
            rhs = QZ[:, kv * 2:(kv + 1) * 2, ns]
            blocks = [(m, "c", m) for m in range(4)]
            if n > 0:
                blocks.append((4, "p", n - 1))
            blocks.append((5, "s", n))
            if n < NT - 1:
                blocks.append((6, "n", n + 1))
            ot, otb = psum((4, 5))
            su, sub = psum((6, 7))

            def qk(bl):
                slot, kind, idx = bl
                st, stb = psum((0, 1, 2, 3))
                lhsT = CK[:, kv, idx * 128:(idx + 1) * 128] if kind == "c" else KT[:, kv, idx * 128:(idx + 1) * 128]
                mm(v4(st[:, 0:256]), lhsT, rhs, True, True, [ktB, qzB[kv][n]], [stb])
                return st, stb

            nxt = qk(blocks[0])
            yield
            for bi, bl in enumerate(blocks):
                slot, kind, idx = bl
                st, stb = nxt
                if bi + 1 < len(blocks):
                    nxt = qk(blocks[bi + 1])
                pi = cnt["it"] % len(PT)
                cnt["it"] += 1
                act(PT[pi][:], st[:, 0:256], AF.Exp, [stb, cB], [ptB[pi]], scale=SCALE,
                    bias=cc("biasB", n * 7 + slot, n * 7 + slot + 1))
                if kind in ("p", "n"):
                    mk = MPT if kind == "p" else MNT
                    tt("pool", v4(PT[pi][:]), v4(PT[pi][:]), bcm(mk[:], 2), ALU.mult, [ptB[pi], mB], [ptB[pi]])
                vl = CV[:, idx, kv * 128:(kv + 1) * 128] if kind == "c" else V[:, idx, kv * 128:(kv + 1) * 128]
                first, last = bi == 0, bi == len(blocks) - 1
                mm(ot[:, 0:256], vl, PT[pi][:], first, last, [ptB[pi], vB], [otb])
                mm(su[:, 0:256], onesb[:], PT[pi][:], first, last, [ptB[pi], cB], [sub])
                yield
            rb = (kv * NT + n) % 2
            for g in range(2):
                hh = i * 4 + kv * 2 + g
                ts("dve", DN[rb][:, g * 128:(g + 1) * 128], su[:, g * 128:(g + 1) * 128], EXS[:, hh:hh + 1], ALU.add,
                   [sub, cB], [rcB[rb]])
            recip(RC[rb][:], DN[rb][:], [rcB[rb]], [rcB[rb]])
            tt("dve", QZ[:, kv * 2:(kv + 1) * 2, ns], v4(ot[:, 0:256]), v4(RC[rb][:]), ALU.mult,
               [otb, rcB[rb]], [qzB[kv][n]])
            yield

        run_window((unit(kv, n) for kv in range(2) for n in range(NT)), 2)
        for h in range(4):
            P.dma("sp", out=Z_d[4 + h], in_=QZ[:, h, :], r=qzB[h // 2], w=[Buf()])

    def gdn_pass(l, i, dr, Ao_dst):
        order = list(range(NT)) if dr == 0 else list(range(NT - 1, -1, -1))
        tri = cc("triF") if dr == 0 else cc("triB")
        Mi = cc("MiF") if dr == 0 else cc("MiB")
        Ms = cc("MsF") if dr == 0 else cc("MsB")
        last = 127 if dr == 0 else 0
        d4 = slice(dr * 4, dr * 4 + 4)

        def t4(name, dt, nb=1):
            ts_ = [sb(name, [128, 4, 128], dt) for _ in range(nb)]
            return ts_, [Buf() for _ in range(nb)]

        QTt, qtB = t4("QTt", BF16, 2)
        KTt, ktB = t4("KTt", BF16, 2)
        VTt, vtB = t4("VTt", BF16, 2)
        OST, ostB = t4("OST", F32, 1)
        KVTM = sb("KVTM", [128, 2, 4, 128], BF16); kvtmB = Buf()
        BA, baB = t4("BA", F32)
        BB, bbB = t4("BB", F32)
        EGB, egbB = t4("EGB", F32)
        QKTb, qktbB = t4("QKTb", BF16)
        Pc, pcB = t4("Pc", CH_DT, 2)
        Qc, qcB = t4("Qc", CH_DT, 2)
        Gc, gcB = t4("Gc", CH_DT, 2)
        TB, tbB = t4("TB", BF16)
        TBE, tbeB = t4("TBE", BF16)
        U, uB = t4("U", F32)
        WT, wtB = t4("WT", BF16)
        QET, qetB = t4("QET", BF16)
        VN, vnB = t4("VN", BF16)
        VN2, vn2B = t4("VN2", BF16)
        S, sB = t4("S", F32)
        Sb, sbB = t4("Sb", BF16)
        SO, soB = t4("SO", F32, 1)
        SM = sb("SM", [128, 8, 4], F32); smB = Buf()
        GCOL, NBc, EGC, BEc, DC0, DCOL = (SM[:, k, :] for k in range(6))
        identf = cc("ident")
        onesf = cc("ones")
        A, aB = BA[0], baB[0]
        Bm, bB = BB[0], bbB[0]

        P.dma("sp", out=S[0][:], in_=s0_d[i, dr].rearrange("h k v -> k h v"), w=[sB[0]])
        cp("dve", Sb[0][:], S[0][:], [sB[0]], [sbB[0]])

        def loads(step):
            n = order[step]
            b = step % 2
            ns = slice(n * 128, (n + 1) * 128)
            P.dma("sp", out=QTt[b][:], in_=Aq_d[:, :, ns].rearrange("h d t -> d h t"), w=[qtB[b]])
            P.dma("sp", out=KTt[b][:], in_=Ak_d[:, :, ns].rearrange("h d t -> d h t"), w=[ktB[b]])
            P.dma("sp", out=VTt[b][:], in_=Av_d[:, :, ns].rearrange("h d t -> d h t"), w=[vtB[b]])

        def hmm(pt, pb, lf, rf, r, start=True, stop=True):
            for h in range(4):
                mm(pt[:, h * 128:(h + 1) * 128], lf(h), rf(h), start, stop, r, [pb])

        def hmm_t(pt_, pb_, src, sB_):
            for h in range(4):
                tr(pt_[:, h * 128:(h + 1) * 128], src[:, h, :], identf, [sB_, cB], [pb_])

        loads(0)
        yield
        for step in range(NT):
            n = order[step]
            b = step % 2
            ns = slice(n * 128, (n + 1) * 128)
            if step + 1 < NT:
                loads(step + 1)
            q_, k_, v_ = QTt[b], KTt[b], VTt[b]
            pt, pb = psum()
            ptb = pt[:].bitcast(BF16)
            for h in range(4):
                tr(ptb[:, h * 128:(h + 1) * 128], k_[:, h, :], identb[:], [ktB[b], cB], [pb])
            for h in range(4):
                tr(ptb[:, 512 + h * 128:512 + (h + 1) * 128], v_[:, h, :], identb[:], [vtB[b], cB], [pb])
            act(KVTM[:].rearrange("p a b c -> p (a b c)"), ptb[:, 0:1024], AF.Copy, [pb], [kvtmB])
            KTM = KVTM[:, 0]
            VTM = KVTM[:, 1]
            pg, pgb_ = psum()
            mm(pg[:, 0:4], tri, GG[:, n, d4], True, True, [cB, bgB], [pgb_])
            cp("dve", GCOL, pg[:, 0:4], [pgb_], [smB])
            tt("dve", A[:], bcm(tri, 4), bcl(GG[:, n, d4], 128), ALU.mult, [cB, bgB], [aB])
            yield
            pgc, pgcb = psum()
            mm(pgc[:], onesf, A[:].rearrange("p a b -> p (a b)"), True, True, [cB, aB], [pgcb])
            for h in range(4):
                ts("dve", A[:, h, :], pgc[:, h * 128:(h + 1) * 128], GCOL[:, h:h + 1], ALU.subtract, [pgcb, smB], [aB],
                   s2=0.0, op1=ALU.max)
            act(A[:], A[:], AF.Exp, [aB], [aB], scale=-1.0)
            act(EGB[0][:].rearrange("p a b -> p (a b)"), pgc[:], AF.Exp, [pgcb], [egbB[0]])
            tt("dve", DC0, v4(pgc[:])[:, :, last], GCOL, ALU.subtract, [pgcb, smB], [smB])
            act(DCOL, DC0, AF.Exp, [smB], [smB])
            act(EGC, GCOL, AF.Exp, [smB], [smB])
            ts("dve", NBc, BETA[:, n, d4], -1.0, ALU.mult, [bgB], [smB])
            tt("dve", BEc, BETA[:, n, d4], EGC, ALU.mult, [bgB, smB], [smB])
            tt("dve", Bm[:], bcm(Ms, 4), bcl(NBc, 128), ALU.mult, [cB, smB], [bB])
            yield
            pkk, pkkb = psum()
            hmm(pkk, pkkb, lambda h: k_[:, h, :], lambda h: k_[:, h, :], [ktB[b]])
            pqk, pqkb = psum()
            hmm(pqk, pqkb, lambda h: q_[:, h, :], lambda h: k_[:, h, :], [ktB[b], qtB[b]])
            stt(Bm[:], A[:], 1.0, Bm[:], ALU.min, ALU.mult, [aB, bB], [bB])
            stt(A[:], A[:], 1.0, bcm(Mi, 4), ALU.min, ALU.mult, [aB, cB], [aB])
            tt("dve", Bm[:], v4(pkk[:]), Bm[:], ALU.mult, [pkkb, bB], [bB])
            tt("dve", A[:], v4(pqk[:]), A[:], ALU.mult, [pqkb, aB], [aB])
            act(Qc[0][:], Bm[:], AF.Copy, [bB], [qcB[0]])
            yield
            pp, ppb = psum()
            hmm_t(pp, ppb, Bm, bB)
            act(Pc[0][:], v4(pp[:]), AF.Copy, [ppb], [pcB[0]])
            tt("dve", Gc[0][:], v4(pp[:]), bcm(identf, 4), ALU.add, [ppb, cB], [gcB[0]])
            pq2, pq2b = psum()
            hmm_t(pq2, pq2b, A, aB)
            act(QKTb[0][:], v4(pq2[:]), AF.Copy, [pq2b], [qktbB[0]])
            tt("pool", QET[0][:], q_[:], EGB[0][:], ALU.mult, [qtB[b], egbB[0]], [qetB[0]])
            yield
            cur = 0
            NLEV = 5
            for j in range(NLEV):
                nx = 1 - cur
                pa, pab = psum()
                hmm(pa, pab, lambda h: Pc[cur][:, h, :], lambda h: Qc[cur][:, h, :], [pcB[cur], qcB[cur]])
                act(Qc[nx][:], v4(pa[:]), AF.Copy, [pab], [qcB[nx]])
                if j < NLEV - 1:
                    pb2, pb2b = psum()
                    hmm(pb2, pb2b, lambda h: Qc[cur][:, h, :], lambda h: Pc[cur][:, h, :], [pcB[cur], qcB[cur]])
                    cp("dve", Pc[nx][:], v4(pb2[:]), [pb2b], [pcB[nx]])
                yield
                pc3, pc3b = psum()
                hmm(pc3, pc3b, lambda h: Qc[nx][:, h, :], lambda h: Gc[cur][:, h, :], [qcB[nx], gcB[cur]])
                tt("dve", Gc[nx][:], Gc[cur][:], v4(pc3[:]), ALU.add, [gcB[cur], pc3b], [gcB[nx]])
                cur = nx
                yield
            Gf, gfB = Gc[cur], gcB[cur]
            pmx, pmxb = psum()
            hmm(pmx, pmxb, lambda h: Bm[:, h, :], lambda h: Gf[:, h, :], [bB, gfB])
            Rt, rB = Pc[0], pcB[0]
            tt("dve", Rt[:], v4(pmx[:]), Gf[:], ALU.subtract, [pmxb, gfB], [rB])
            tt("pool", Rt[:], Rt[:], bcm(identf, 4), ALU.add, [rB, cB], [rB])
            pxt, pxtb = psum()
            hmm_t(pxt, pxtb, Gf, gfB)
            act(Qc[0][:], v4(pxt[:]), AF.Copy, [pxtb], [qcB[0]])
            yield
            pxr, pxrb = psum()
            hmm(pxr, pxrb, lambda h: Qc[0][:, h, :], lambda h: Rt[:, h, :], [qcB[0], rB])
            Gn, gnB = Gc[1 - cur], gcB[1 - cur]
            tt("dve", Gn[:], Gf[:], v4(pxr[:]), ALU.add, [gfB, pxrb], [gnB])
            Gf, gfB = Gn, gnB
            yield
            tt("dve", TB[0][:], Gf[:], bcl(BETA[:, n, d4], 128), ALU.mult, [gfB, bgB], [tbB[0]])
            tt("pool", TBE[0][:], Gf[:], bcl(BEc, 128), ALU.mult, [gfB, smB], [tbeB[0]])
            pu, pub = psum()
            hmm(pu, pub, lambda h: TB[0][:, h, :], lambda h: VTM[:, h, :], [tbB[0], kvtmB])
            act(U[0][:], v4(pu[:]), AF.Copy, [pub], [uB[0]])
            pw, pwb = psum()
            hmm(pw, pwb, lambda h: KTM[:, h, :], lambda h: TBE[0][:, h, :], [tbeB[0], kvtmB])
            act(WT[0][:], v4(pw[:]), AF.Copy, [pwb], [wtB[0]])
            yield
            if step > 0 and step % 2 == 0:
                ts("dve", S[0][:], S[0][:], cc("flag"), ALU.mult, [sB[0], cB], [sB[0]])
                ts("dve", Sb[0][:], Sb[0][:], cc("flag"), ALU.mult, [sbB[0], cB], [sbB[0]])
            pws, pwsb = psum()
            hmm(pws, pwsb, lambda h: WT[0][:, h, :], lambda h: Sb[0][:, h, :], [wtB[0], sbB[0]])
            tt("dve", VN[0][:], U[0][:], v4(pws[:]), ALU.subtract, [uB[0], pwsb], [vnB[0]])
            tt("pool", VN2[0][:], VN[0][:], bcl(DCOL, 128), ALU.mult, [vnB[0], smB], [vn2B[0]])
            yield
            po, pob = psum()
            for h in range(4):
                hs = slice(h * 128, (h + 1) * 128)
                mm(po[:, hs], Sb[0][:, h, :], QET[0][:, h, :], True, False, [sbB[0], qetB[0]], [pob])
                mm(po[:, hs], VN[0][:, h, :], QKTb[0][:, h, :], False, True, [vnB[0], qktbB[0]], [pob])
            act(OST[0][:], v4(po[:]), AF.Copy, [pob], [ostB[0]])
            P.dma("sp", out=Ao_dst[:, :, ns].rearrange("h d t -> d h t"), in_=OST[0][:], r=[ostB[0]], w=[Buf()])
            pds, pdsb = psum()
            hmm(pds, pdsb, lambda h: KTM[:, h, :], lambda h: VN2[0][:, h, :], [kvtmB, vn2B[0]])
            yield
            for h in range(4):
                stt(S[0][:, h, :], S[0][:, h, :], EGB[0][:, h, last:last + 1], pds[:, h * 128:(h + 1) * 128],
                    ALU.mult, ALU.add, [sB[0], egbB[0], pdsb], [sB[0]])
            act(Sb[0][:], S[0][:], AF.Copy, [sB[0]], [sbB[0]])
            if step % 2 == 1:
                seg = n // 2
                k2 = 0
                cp("pool", SO[k2][:], S[0][:], [sB[0]], [soB[k2]])
                P.dma("sp", out=ost_d[i, seg, dr].rearrange("h k v -> k h v"), in_=SO[k2][:], r=[soB[k2]], w=[Buf()])
            yield

    def run_interleaved(gens):
        gens = list(gens)
        if SEQ_GDN:
            for g in gens:
                for _ in g:
                    pass
            return
        while gens:
            for g in list(gens):
                try:
                    next(g)
                except StopIteration:
                    gens.remove(g)

    def gdn_combine(l, i):
        def t4(name, dt, nb=1):
            ts_ = [sb(name, [128, 4, 128], dt) for _ in range(nb)]
            return ts_, [Buf() for _ in range(nb)]
        NBUF = 4
        OF, ofB = t4("OF", F32, NBUF)
        OB, obB = t4("OB", F32, NBUF)
        GT, gtB = t4("GT", BF16, NBUF)
        ZA, zaB = t4("ZA", BF16, NBUF)
        O_, oB = t4("O_", F32, NBUF)
        SQo, sqoB = t4("SQo", BF16, NBUF)
        RSo, rsoB = t4("RSo", F32, NBUF)
        R0o, r0oB = t4("R0o", F32, NBUF)

        def loads(n):
            b = n % NBUF
            ns = slice(n * 128, (n + 1) * 128)
            P.dma("sp", out=OF[b][:], in_=Aof_d[:, :, ns].rearrange("h d t -> d h t"), w=[ofB[b]])
            P.dma("sp", out=OB[b][:], in_=Ao_d[:, :, ns].rearrange("h d t -> d h t"), w=[obB[b]])
            P.dma("sp", out=GT[b][:], in_=Ag_d[:, :, ns].rearrange("h d t -> d h t"), w=[gtB[b]])

        def unit(n):
            b = n % NBUF
            ns = slice(n * 128, (n + 1) * 128)
            loads(n)
            yield
            tt("pool", O_[b][:], OF[b][:], OB[b][:], ALU.add, [ofB[b], obB[b]], [oB[b]])
            yield
            act(SQo[b][:], O_[b][:], AF.Square, [oB[b]], [sqoB[b]])
            yield
            pss, pssb = psum()
            mm(pss[:], onesb[:], SQo[b][:].rearrange("p a b -> p (a b)"), True, True, [sqoB[b], cB], [pssb])
            yield
            act(R0o[b][:].rearrange("p a b -> p (a b)"), pss[:], AF.Ln, [pssb, cB], [r0oB[b]], scale=1.0 / 128,
                bias=epsc[:, 0:1])
            yield
            act(RSo[b][:], R0o[b][:], AF.Exp, [r0oB[b]], [rsoB[b]], scale=-0.5)
            yield
            stt(O_[b][:], O_[b][:], pc(("ang", i)), RSo[b][:], ALU.mult, ALU.mult, [oB[b], rsoB[b], cB], [oB[b]])
            yield
            tt("pool", ZA[b][:], O_[b][:], GT[b][:], ALU.mult, [oB[b], gtB[b]], [zaB[b]])
            P.dma("sp", out=Z_d[0:4, :, ns].rearrange("h d t -> d h t"), in_=ZA[b][:], r=[zaB[b]], w=[Buf()])
            yield

        run_window((unit(n) for n in range(NT)), NBUF)

    def HBf(HB):
        return lambda kc, tc: [HB[kc][tc]]

    def make_HT():
        HT = sb("HT", [128, 8, T], BF16)
        HB = [[Buf() for _ in range(4)] for _ in range(8)]
        return HT, HB

    P.mark('adaln0')
    with phase():
        ring_alloc()
        adaln(0)
    for l in range(DEPTH):
        i = l // 2
        do_mixer = (l % 2 == 0 and DO_EVEN) or (l % 2 == 1 and DO_ODD)
        if do_mixer:
            with phase():
                HT, HB = make_HT()
                with phase():
                    P.mark('L%d norm1' % l)
                    norm_mod(A1[:, l, :], MOD[:, l, 0:8], lambda dc, tc: HT[:, dc, tc * 512:(tc + 1) * 512],
                             lambda dc, tc: HB[dc][tc], modB[l], norm_locals(),
                             dst_all=lambda tc: HT[:, :, tc * 512:(tc + 1) * 512])
                if l % 2 == 1:
                    with phase():
                        P.mark('L%d odd_inproj' % l)
                        ring_alloc()
                        odd_inproj(l, i, HT, HBf(HB))
                else:
                    with phase():
                        P.mark('L%d even_inprojA' % l)
                        ring_alloc()
                        even_inprojA(l, i, HT, HBf(HB))
                    with phase():
                        P.mark('L%d even_inprojB' % l)
                        ring_alloc()
                        even_inprojB(l, i, HT, HBf(HB))
            if l % 2 == 1:
                with phase():
                    P.mark('L%d odd_attn' % l)
                    ring_alloc()
                    odd_attn(l, i)
            else:
                with phase():
                    P.mark('L%d even_attnB' % l)
                    even_attnB(l, i)
                with phase():
                    P.mark('L%d gdn' % l)
                    g_b, g_f = gdn_pass(l, i, 1, Ao_d), gdn_pass(l, i, 0, Aof_d)
                    for _ in range(10):
                        next(g_b)
                    run_interleaved([g_b, g_f])
                with phase():
                    P.mark('L%d gdn_comb' % l)
                    gdn_combine(l, i)
                with phase():
                    P.mark('L%d even_outproj' % l)
                    ring_alloc()
                    Z = sb("Z", [128, 8, T], BF16)
                    zB = [Buf() for _ in range(8)]
                    for zc in range(8):
                        P.dma("sp", out=Z[:, zc, :], in_=Z_d[zc], w=[zB[zc]])
                    out_proj(l, ("evO", i), Z, lambda zc, tc: [zB[zc]])
        with phase():
            HT, HB = make_HT()
            with phase():
                P.mark('L%d norm2' % l)
                norm_mod(A2[:, l, :], MOD[:, l, 24:32], lambda dc, tc: HT[:, dc, tc * 512:(tc + 1) * 512],
                         lambda dc, tc: HB[dc][tc], modB[l], norm_locals(),
                         dst_all=lambda tc: HT[:, :, tc * 512:(tc + 1) * 512])
            P.mark('L%d mlp' % l)
            ring_alloc()
            mlp(l, HT, HBf(HB))
    with phase():
        P.mark('final')
        YT = sb("YT", [128, 8, 512], F32)
        yB = [Buf() for _ in range(8)]
        norm_mod(pc("fing"), zero8[:], lambda dc, tc: YT[:, dc, :], lambda dc, tc: yB[dc], cB, norm_locals(),
                 post=lambda dc, tc: P.dma("sp", out=yT_d[dc * 128:(dc + 1) * 128, tc * 512:(tc + 1) * 512],
                                           in_=YT[:, dc, :], r=[yB[dc]], w=[Buf()]),
                 dst_all=lambda tc: YT[:])
    P.barrier()
    global LAST_PROG
    LAST_PROG = P
    return nc


def _rope_tables():
    t = np.arange(T)
    row = (t // 64).astype(np.float32)
    col = (t % 64).astype(np.float32)
    inv = (10000.0 ** (-np.arange(32, dtype=np.float32) / 32)).astype(np.float32)
    cos = np.zeros((128, T), np.float32)
    sin = np.zeros((128, T), np.float32)
    for a, pos in enumerate((row, col)):
        ang = pos[None, :] * inv[:, None]
        for half in range(2):
            cos[a * 64 + half * 32:a * 64 + half * 32 + 32] = np.cos(ang)
            sin[a * 64 + half * 32:a * 64 + half * 32 + 32] = np.sin(ang)
    return cos, sin


def _consts(is_sample):
    cplan, NCST = cst_plan()
    C = np.zeros((128, NCST), np.float32)

    def put(name, a):
        o, w = cplan[name]
        C[:, o:o + w] = a

    idx = np.arange(128)
    k = idx[:, None]
    c = idx[None, :]
    put("ident", np.eye(128, dtype=np.float32))
    put("ones", np.ones((128, 128), np.float32))
    put("triF", (k <= c).astype(np.float32))
    put("triB", (k >= c).astype(np.float32))
    put("MiF", (c <= k).astype(np.float32))
    put("MsF", (c < k).astype(np.float32))
    put("MiB", (c >= k).astype(np.float32))
    put("MsB", (c > k).astype(np.float32))
    Rm = np.zeros((128, 128), np.float32)
    for a in range(2):
        for f in range(32):
            Rm[a * 64 + 32 + f, a * 64 + f] = -1.0
            Rm[a * 64 + f, a * 64 + 32 + f] = 1.0
    put("Rm", Rm)
    biasB = np.zeros((16, 7), np.float32)
    biasC = np.zeros((16, 20), np.float32)
    if is_sample:
        cos, sin = _rope_tables()
        put("mPT", (k >= c).astype(np.float32))
        put("mNT", (k <= c).astype(np.float32))
        flag, nflag = 1.0, 0.0
    else:
        cos = np.ones((128, T), np.float32)
        sin = np.zeros((128, T), np.float32)
        put("mPT", np.ones((128, 128), np.float32))
        put("mNT", np.ones((128, 128), np.float32))
        biasB[:, 0:4] = NEGB
        biasC[:, 0:4] = NEGB
        for n in range(16):
            if n % 2 == 0:
                biasB[n, 4] = NEGB
            else:
                biasB[n, 6] = NEGB
            for m in range(16):
                if m // 2 != n // 2:
                    biasC[n, 4 + m] = NEGB
        flag, nflag = 0.0, 1.0
    put("cos", cos)
    put("sin", sin)
    put("biasB", np.broadcast_to(biasB.reshape(1, -1), (128, 112)))
    put("biasC", np.broadcast_to(biasC.reshape(1, -1), (128, 320)))
    put("flag", flag)
    put("nflag", nflag)
    return C


def _col8(v):
    return np.ascontiguousarray(v.reshape(8, 128).T)


_NC_CACHE = {}


def kernel(x_prompt, x_sample, state_a, cache_b_kv, cache_c_kv, c, c_ctx, ada_w, ada_b, norm1_g, norm2_g,
           final_g, mlp_w1, mlp_w2, ev_w_in, a_conv, a_log, a_dt_bias, a_norm_g, b_sink, ev_w_out,
           od_w_in, c_qnorm_g, c_knorm_g, od_w_out):
    f = lambda a: np.asarray(a, dtype=np.float32)
    x_prompt, x_sample, state_a, cache_b_kv, cache_c_kv = map(f, (x_prompt, x_sample, state_a, cache_b_kv, cache_c_kv))
    c, c_ctx, ada_w, ada_b, norm1_g, norm2_g, final_g = map(f, (c, c_ctx, ada_w, ada_b, norm1_g, norm2_g, final_g))
    mlp_w1, mlp_w2, ev_w_in, a_conv, a_log, a_dt_bias, a_norm_g, b_sink, ev_w_out = map(
        f, (mlp_w1, mlp_w2, ev_w_in, a_conv, a_log, a_dt_bias, a_norm_g, b_sink, ev_w_out))
    od_w_in, c_qnorm_g, c_knorm_g, od_w_out = map(f, (od_w_in, c_qnorm_g, c_knorm_g, od_w_out))

    wplan, NPIECE = weight_plan()
    pplan, NPRM = prm_plan()
    wts = np.empty((NPIECE, 128, 2048), np.float32)

    def putw(key, arr):
        p0, n = wplan[key]
        assert arr.shape[0] == n, (key, arr.shape, n)
        wts[p0:p0 + n] = arr

    for l in range(DEPTH):
        putw(("ada", l), _pack(ada_w[l], 8, 256))
        putw(("w1", l), _pack(mlp_w1[l], 8, 256))
        putw(("w2", l), _pack(mlp_w2[l], 4, 512))
    for i in range(2):
        putw(("evA", i), _pack(ev_w_in[i][:, 0:2048], 8, 256))
        putw(("evB", i), _pack(ev_w_in[i][:, 2064:3088], 8, 256))
        putw(("evO", i), _pack(ev_w_out[i], 8, 256))
        putw(("odI", i), _pack(od_w_in[i], 8, 256))
        putw(("odO", i), _pack(od_w_out[i], 8, 256))
    wbg = np.stack([_pack(ev_w_in[i][:, 2048:2064], 8, 16)[0] for i in range(2)])

    def prm_for(cond):
        Pm = np.zeros((128, NPRM), np.float32)

        def put(name, a):
            o, w = pplan[name]
            Pm[:, o:o + w] = a

        put("cond", _col8(cond))
        for l in range(DEPTH):
            put(("ada_b", l), np.ascontiguousarray(ada_b[l].reshape(48, 128).T))
            put(("n1g", l), _col8(norm1_g[l]))
            put(("n2g", l), _col8(norm2_g[l]))
        put("fing", _col8(final_g))
        for i in range(2):
            cw = a_conv[i].reshape(3, 12, 128).transpose(2, 1, 0).reshape(128, 36)
            put(("conv", i), cw)
            put(("alog", i), np.broadcast_to(a_log[i].reshape(1, 8), (128, 8)))
            put(("dtb", i), np.broadcast_to(a_dt_bias[i].reshape(1, 8), (128, 8)))
            put(("ang", i), a_norm_g[i].reshape(128, 1))
            put(("sink", i), np.broadcast_to(b_sink[i].reshape(1, 4), (128, 4)))
            put(("cqg", i), c_qnorm_g[i].reshape(128, 1))
            put(("ckg", i), c_knorm_g[i].reshape(128, 1))
            put(("ckgb", i), np.broadcast_to(c_knorm_g[i].reshape(1, 128), (128, 128)))
        return Pm

    cst_s = _consts(True)
    cst_p = _consts(False)
    zeros_s0 = np.zeros((2, 2, 4, 128, 128), np.float32)
    zeros_ck = np.zeros((2, 2, 128, 512), np.float32)
    zeros_cv = np.zeros((2, 512, 256), np.float32)
    in_maps = []
    for core in range(8):
        if core < 4:
            b = core
            m = {
                "xT": np.ascontiguousarray(x_sample[b].T),
                "prm": prm_for(c[b]),
                "cst": cst_s,
                "s0": np.ascontiguousarray(state_a[b]),
                "ckb": np.ascontiguousarray(cache_b_kv[b, :, 0].transpose(0, 2, 3, 1)),
                "cvb": np.ascontiguousarray(cache_b_kv[b, :, 1].reshape(2, 512, 256)),
                "ckc": np.ascontiguousarray(cache_c_kv[b, :, 0].transpose(0, 2, 3, 1)),
                "cvc": np.ascontiguousarray(cache_c_kv[b, :, 1].reshape(2, 512, 256)),
            }
        else:
            pcid = (core - 4) % 2
            xs = x_prompt[pcid * 8:(pcid + 1) * 8].reshape(T, D)
            m = {
                "xT": np.ascontiguousarray(xs.T),
                "prm": prm_for(c_ctx),
                "cst": cst_p,
                "s0": zeros_s0, "ckb": zeros_ck, "cvb": zeros_cv, "ckc": zeros_ck, "cvc": zeros_cv,
            }
        m["wts"] = wts
        m["wbg"] = wbg
        in_maps.append(m)

    if "nc" not in _NC_CACHE:
        _NC_CACHE["nc"] = build_program()
    nc = _NC_CACHE["nc"]
    res = run_bass_kernel_spmd(nc, in_maps, core_ids=list(range(8)))
    R = res.results
    y_sample = np.stack([R[b]["yT"].T for b in range(4)])
    y_prompt = np.concatenate([R[4 + p]["yT"].T.reshape(8, 256, D) for p in range(2)])
    new_a = np.concatenate([R[4 + p]["ost"].transpose(1, 0, 2, 3, 4, 5) for p in range(2)])
    nb_k = np.concatenate([R[4 + p]["obk"].transpose(3, 0, 1, 2).reshape(8, 256, 2, 2, 128).transpose(0, 2, 1, 3, 4)
                           for p in range(2)])
    nb_v = np.concatenate([R[4 + p]["obv"].reshape(2, 8, 256, 2, 128).transpose(1, 0, 2, 3, 4) for p in range(2)])
    new_b = np.stack([nb_k, nb_v], axis=2)
    nc_k = np.concatenate([R[4 + p]["ock"].reshape(2, 8, 256, 2, 128).transpose(1, 0, 2, 3, 4) for p in range(2)])
    nc_v = np.concatenate([R[4 + p]["ocv"].reshape(2, 8, 256, 2, 128).transpose(1, 0, 2, 3, 4) for p in range(2)])
    new_c = np.stack([nc_k, nc_v], axis=2)
    return (np.ascontiguousarray(y_prompt, dtype=np.float32), np.ascontiguousarray(y_sample, dtype=np.float32),
            np.ascontiguousarray(new_a, dtype=np.float32), np.ascontiguousarray(new_b, dtype=np.float32),
            np.ascontiguousarray(new_c, dtype=np.float32))
```

```python
import numpy as np
import concourse.bass as bass
import concourse.mybir as mybir
from concourse.bass_utils import run_bass_kernel_spmd

F32, BF16 = mybir.dt.float32, mybir.dt.bfloat16
AF = mybir.ActivationFunctionType
ALU = mybir.AluOpType

D = 1024
T = 2048
NT = 16
DEPTH = 4
EPS = 1e-6
NEGB = -30000.0
SCALE = 128 ** -0.5
CH_DT = F32

DO_EVEN = True
DO_ODD = True
DO_GDN = True
DO_BATT = True
LAST_PROG = None
SEQ_GDN = False


class Buf:
    __slots__ = ("w", "r", "name", "excl")

    def __init__(self, name="", excl=False):
        self.w = None
        self.r = {}
        self.name = name
        self.excl = excl


class Prog:
    def __init__(self, nc, n_sp=24, n_pool=14):
        self.nc = nc
        self.E = {"pe": nc.tensor, "act": nc.scalar, "dve": nc.vector, "pool": nc.gpsimd, "sp": nc.sync}
        self.semh = {k: nc.alloc_semaphore("s_" + k) for k in self.E}
        self.cnt = {k: 0 for k in self.E}
        self.seen = {k: {} for k in self.E}
        self.dq = {"sp": [("dsp", i) for i in range(n_sp)], "pool": [("dpl", i) for i in range(n_pool)]}
        for q in self.dq:
            for key in self.dq[q]:
                self.semh[key] = nc.alloc_semaphore("%s%d" % key)
                self.cnt[key] = 0
        self.dq_next = {"sp": 0, "pool": 0}
        self.nins = 0
        self.glob = 0
        self.marks = []

    def _wait(self, eng, need):
        for k, v in need.items():
            if k == eng and eng == "pe":
                continue
            if self.seen[eng].get(k, 0) < v:
                self.E[eng].wait_ge(self.semh[k], v)
                self.glob += 1
                self.seen[eng][k] = v

    @staticmethod
    def _deps(r, w, need, eng=None):
        for b in r:
            if b.w is not None:
                k, v = b.w
                if need.get(k, 0) < v:
                    need[k] = v
            if b.excl:
                for k, v in b.r.items():
                    if k != eng and need.get(k, 0) < v:
                        need[k] = v
        for b in w:
            if b.w is not None:
                k, v = b.w
                if need.get(k, 0) < v:
                    need[k] = v
            for k, v in b.r.items():
                if need.get(k, 0) < v:
                    need[k] = v

    @staticmethod
    def _mark(r, w, tok):
        for b in w:
            b.w = tok
            b.r = {}
        for b in r:
            b.r[tok[0]] = tok[1]

    def op(self, eng, fn, r=(), w=()):
        need = {}
        self._deps(r, w, need, eng)
        self._wait(eng, need)
        ins = fn(self.E[eng])
        self.glob += 1
        self.cnt[eng] += 1
        ins.then_inc(self.semh[eng], 1)
        self._mark(r, w, (eng, self.cnt[eng]))
        self.nins += 1

    def dma(self, q, out, in_, r=(), w=(), **kw):
        i = self.dq_next[q]
        self.dq_next[q] = (i + 1) % len(self.dq[q])
        key = self.dq[q][i]
        need = {}
        self._deps(r, w, need)
        if self.cnt[key]:
            need[key] = max(need.get(key, 0), self.cnt[key])
        self._wait(q, need)
        ins = self.E[q].dma_start(out=out, in_=in_, **kw)
        self.glob += 1
        self.cnt[key] += 16
        ins.then_inc(self.semh[key], 16)
        self._mark(r, w, (key, self.cnt[key]))
        self.nins += 1

    def mark(self, name):
        self.marks.append((name, self.glob))

    def barrier(self):
        for e in ("pe", "act", "dve", "pool", "sp"):
            need = {k: v for k, v in self.cnt.items() if v > 0 and (k != e or e != "pe")}
            self._wait(e, need)


def weight_plan():
    plan = {}
    n = 0
    for l in range(DEPTH):
        plan[("ada", l)] = (n, 24); n += 24
        plan[("w1", l)] = (n, 16); n += 16
        plan[("w2", l)] = (n, 16); n += 16
        if l % 2 == 0:
            plan[("evA", l // 2)] = (n, 8); n += 8
            plan[("evB", l // 2)] = (n, 4); n += 4
            plan[("evO", l // 2)] = (n, 4); n += 4
        else:
            plan[("odI", l // 2)] = (n, 6); n += 6
            plan[("odO", l // 2)] = (n, 4); n += 4
    return plan, n


def _pack(W, kpp, cpp):
    K, N = W.shape
    kc = K // 128
    Wr = W.reshape(kc // kpp, kpp, 128, N // cpp, cpp)
    return np.ascontiguousarray(Wr.transpose(0, 3, 2, 1, 4)).reshape(-1, 128, kpp * cpp)


def prm_plan():
    m = {}
    n = 0

    def add(name, w):
        nonlocal n
        m[name] = (n, w)
        n += w

    add("cond", 8)
    for l in range(DEPTH):
        add(("ada_b", l), 48)
        add(("n1g", l), 8)
        add(("n2g", l), 8)
    add("fing", 8)
    for i in range(2):
        add(("conv", i), 36)
        add(("alog", i), 8)
        add(("dtb", i), 8)
        add(("ang", i), 1)
        add(("sink", i), 4)
        add(("cqg", i), 1)
        add(("ckg", i), 1)
        add(("ckgb", i), 128)
    return m, n


def cst_plan():
    m = {}
    n = 0

    def add(name, w):
        nonlocal n
        m[name] = (n, w)
        n += w

    for nm in ("ident", "ones", "triF", "triB", "MiF", "MsF", "MiB", "MsB", "Rm", "mPT", "mNT"):
        add(nm, 128)
    add("cos", T)
    add("sin", T)
    add("biasB", 16 * 7)
    add("biasC", 16 * 20)
    add("flag", 1)
    add("nflag", 1)
    return m, n


def build_program(debug=False):
    nc = bass.Bass("TRN2", target_bir_lowering=False)
    P = Prog(nc)
    wplan, NPIECE = weight_plan()
    pplan, NPRM = prm_plan()
    cplan, NCST = cst_plan()

    def din(name, shape, dt=F32):
        return nc.dram_tensor(name, list(shape), dt, kind="ExternalInput").ap()

    def dout(name, shape, dt=F32):
        return nc.dram_tensor(name, list(shape), dt, kind="ExternalOutput").ap()

    def dscr(name, shape, dt):
        return nc.dram_tensor(name, list(shape), dt, kind="Internal").ap()

    xT_d = din("xT", [D, T])
    wts_d = din("wts", [NPIECE, 128, 2048])
    wbg_d = din("wbg", [2, 128, 8 * 16])
    prm_d = din("prm", [128, NPRM])
    cst_d = din("cst", [128, NCST])
    s0_d = din("s0", [2, 2, 4, 128, 128])
    ckb_d = din("ckb", [2, 2, 128, 512])
    cvb_d = din("cvb", [2, 512, 256])
    ckc_d = din("ckc", [2, 2, 128, 512])
    cvc_d = din("cvc", [2, 512, 256])

    yT_d = dout("yT", [D, T])
    ost_d = dout("ost", [2, 8, 2, 4, 128, 128])
    obk_d = dout("obk", [2, 2, 128, T])
    obv_d = dout("obv", [2, T, 256])
    ock_d = dout("ock", [2, T, 256])
    ocv_d = dout("ocv", [2, T, 256])

    Aq_d = dscr("Aq", [4, 128, T], BF16)
    Ak_d = dscr("Ak", [4, 128, T], BF16)
    Av_d = dscr("Av", [4, 128, T], BF16)
    Ag_d = dscr("Ag", [4, 128, T], BF16)
    Ao_d = dscr("Ao", [4, 128, T], F32)
    Aof_d = dscr("Aof", [4, 128, T], F32)
    Z_d = dscr("Zs", [8, 128, T], BF16)
    Q_d = dscr("Qs", [8, 128, T], BF16)
    K_d = dscr("Ks", [2, 128, T], BF16)
    V_d = dscr("Vs", [T, 256], BF16)
    scrB = {k: Buf(k) for k in ("Aq", "Ak", "Av", "Ag", "Ao", "Z", "Q", "K", "V")}

    from contextlib import ExitStack, contextmanager
    stacks = [ExitStack()]
    uniq = {"n": 0}

    def sb(name, shape, dt):
        uniq["n"] += 1
        return stacks[-1].enter_context(nc.sbuf_tensor("%s_%d" % (name, uniq["n"]), list(shape), dt))

    @contextmanager
    def phase():
        st = ExitStack()
        stacks.append(st)
        try:
            yield
        finally:
            P.barrier()
            stacks.pop()
            st.close()

    XT = sb("XT", [128, 8, T], F32)
    XB = [[Buf("XT%d_%d" % (dc, tc)) for tc in range(4)] for dc in range(8)]
    CST = sb("CST", [128, NCST], F32)
    PRM = sb("PRM", [128, NPRM], F32)
    cB = Buf("cst")
    NSLOT = 5
    ring = {"WS": None, "WSB": None, "i": 0}
    ring["WS"] = [sb("WS%d" % i, [128, 2048], BF16) for i in range(NSLOT)]
    ring["WSB"] = [Buf("WS%d" % i) for i in range(NSLOT)]

    def ring_alloc():
        pass
    identb = sb("identb", [128, 128], BF16)
    onesb = sb("onesb", [128, 128], BF16)
    MOD = sb("MOD", [128, DEPTH, 48], F32)
    A1 = sb("A1", [128, DEPTH, 8], F32)
    A2 = sb("A2", [128, DEPTH, 8], F32)
    modB = [Buf("mod%d" % l) for l in range(DEPTH)]
    scb = sb("scb", [128, 8], BF16)
    zero8 = sb("zero8", [128, 8], F32)
    WBG = sb("WBG", [128, 2, 8 * 16], BF16)
    BETA = sb("BETA", [128, NT, 8], F32)
    GG = sb("GG", [128, NT, 8], F32)
    bgB = Buf("bg")
    EXS = sb("EXS", [128, 8], F32)
    NEA = sb("NEA", [128, 16], F32)

    PSB = [nc.alloc_psum_tensor("ps%d" % i, [128, 512], F32) for i in range(8)]
    PB = [Buf("ps%d" % i, excl=True) for i in range(8)]

    def cc(name, lo=0, hi=None):
        o, w = cplan[name]
        hi = w if hi is None else hi
        return CST[:, o + lo:o + hi]

    def pc(name, lo=0, hi=None):
        o, w = pplan[name]
        hi = w if hi is None else hi
        return PRM[:, o + lo:o + hi]

    def mm(out, lhsT, rhs, start, stop, r, w):
        P.op("pe", lambda e: e.matmul(out, lhsT=lhsT, rhs=rhs, start=start, stop=stop), r, w)

    def tr(out, in_, ident, r, w):
        P.op("pe", lambda e: e.transpose(out=out, in_=in_, identity=ident), r, w)

    def act(out, in_, func, r, w, **kw):
        P.op("act", lambda e: e.activation(out=out, in_=in_, func=func, **kw), r, w)

    def tt(eng, out, in0, in1, op, r, w):
        P.op(eng, lambda e: e.tensor_tensor(out=out, in0=in0, in1=in1, op=op), r, w)

    def ts(eng, out, in0, s1, op0, r, w, s2=None, op1=None):
        if op1 is None:
            P.op(eng, lambda e: e.tensor_scalar(out=out, in0=in0, scalar1=s1, scalar2=None, op0=op0), r, w)
        else:
            P.op(eng, lambda e: e.tensor_scalar(out=out, in0=in0, scalar1=s1, scalar2=s2, op0=op0, op1=op1), r, w)

    def stt(out, in0, scalar, in1, op0, op1, r, w):
        P.op("dve", lambda e: e.scalar_tensor_tensor(out=out, in0=in0, scalar=scalar, in1=in1, op0=op0, op1=op1), r, w)

    def cp(eng, out, in_, r, w):
        P.op(eng, lambda e: e.tensor_copy(out=out, in_=in_), r, w)

    def recip(out, in_, r, w):
        act(out, in_, AF.Ln, r, w)
        act(out, out, AF.Exp, list(w), w, scale=-1.0)

    def rsqrt(out, in_, scale, eps, tmp, r, w, tmpB):
        act(tmp, in_, AF.Ln, list(r) + [cB], [tmpB], scale=scale, bias=epsc[:, 0:1])
        act(out, tmp, AF.Exp, [tmpB], w, scale=-0.5)

    def wload(piece):
        k = ring["i"]
        ring["i"] = (k + 1) % NSLOT
        WS, WSB = ring["WS"], ring["WSB"]
        P.dma("pool", out=WS[k][:], in_=wts_d[piece], r=[], w=[WSB[k]])
        return WS[k], WSB[k]

    pstate = {}

    def psum(banks=(0, 1, 2, 3, 4, 5, 6)):
        i = pstate.get(banks, 0)
        pstate[banks] = i + 1
        b = banks[i % len(banks)]
        return PSB[b], PB[b]

    def run_window(gen_iter, width=2):
        gen_iter = iter(gen_iter)
        active = []
        done = False
        while True:
            while not done and len(active) < width:
                try:
                    active.append(next(gen_iter))
                except StopIteration:
                    done = True
            if not active:
                break
            for g in list(active):
                try:
                    next(g)
                except StopIteration:
                    active.remove(g)

    def bcl(ap2, n):
        return ap2.unsqueeze(2).to_broadcast([128, ap2.shape[1], n])

    def bcm(ap2, k):
        return ap2.unsqueeze(1).to_broadcast([128, k, ap2.shape[1]])

    def v4(ap):
        return ap.rearrange("p (a b) -> p a b", b=128)

    P.dma("sp", out=CST[:], in_=cst_d[:, :], w=[cB])
    P.dma("sp", out=PRM[:], in_=prm_d[:, :], w=[cB])
    for dc in range(8):
        P.dma("sp", out=XT[:, dc, :], in_=xT_d[dc * 128:(dc + 1) * 128, :], w=XB[dc])
    P.dma("pool", out=WBG[:], in_=wbg_d.rearrange("i p c -> p i c"), w=[cB])
    epsc = sb("epsc", [128, 1], F32)
    P.op("dve", lambda e: e.memset(epsc[:], EPS), [], [cB])
    P.op("dve", lambda e: e.memset(zero8[:], 0.0), [], [cB])
    cp("dve", identb[:], cc("ident"), [cB], [cB])
    cp("dve", onesb[:], cc("ones"), [cB], [cB])
    act(scb[:], pc("cond"), AF.Silu, [cB], [cB])
    for i in range(2):
        act(EXS[:, i * 4:(i + 1) * 4], pc(("sink", i)), AF.Exp, [cB], [cB])
        act(NEA[:, i * 8:(i + 1) * 8], pc(("alog", i)), AF.Exp, [cB], [cB])
    ts("dve", NEA[:], NEA[:], -1.0, ALU.mult, [cB], [cB])

    def adaln_piece(l, pi):
        p0, npc = wplan[("ada", l)]
        pt, pb = PSB[7], PB[7]
        wt, wb = wload(p0 + pi)
        wv = wt[:].rearrange("p (k c) -> p k c", k=8)
        for ft in range(2):
            j = pi * 2 + ft
            for kc in range(8):
                mm(pt[:, j:j + 1], wv[:, kc, ft * 128:(ft + 1) * 128], scb[:, kc:kc + 1], kc == 0, kc == 7,
                   [wb, cB], [pb])

    def adaln_finish(l):
        pt, pb = PSB[7], PB[7]
        tt("dve", MOD[:, l, :], pt[:, 0:48], pc(("ada_b", l)), ALU.add, [pb, cB], [modB[l]])
        ts("dve", A1[:, l, :], MOD[:, l, 8:16], 1.0, ALU.add, [modB[l]], [modB[l]])
        tt("dve", A1[:, l, :], A1[:, l, :], pc(("n1g", l)), ALU.mult, [modB[l], cB], [modB[l]])
        ts("dve", A2[:, l, :], MOD[:, l, 32:40], 1.0, ALU.add, [modB[l]], [modB[l]])
        tt("dve", A2[:, l, :], A2[:, l, :], pc(("n2g", l)), ALU.mult, [modB[l], cB], [modB[l]])

    def adaln(l):
        for pi in range(24):
            adaln_piece(l, pi)
        adaln_finish(l)

    def norm_mod(A, SH, dstf, dstB, lB, loc, post=None, dst_all=None):
        def chain(tc):
            SQ, sqB, RS, rsB, TM, tmB, R0, r0B = loc[tc % 2]
            cs = slice(tc * 512, (tc + 1) * 512)
            act(SQ[:], XT[:, :, cs], AF.Square, [XB[dc][tc] for dc in range(8)], [sqB])
            yield
            pt, pb = psum((4, 5, 6))
            for dc in range(8):
                mm(pt[:], onesb[:], SQ[:, dc, :], dc == 0, dc == 7, [sqB, cB], [pb])
            yield
            act(R0[:], pt[:], AF.Ln, [pb, cB], [r0B], scale=1.0 / D, bias=epsc[:, 0:1])
            yield
            act(RS[:], R0[:], AF.Exp, [r0B], [rsB], scale=-0.5)
            yield
            for dc in range(8):
                stt(TM[:, dc, :], XT[:, dc, cs], A[:, dc:dc + 1], RS[:], ALU.mult, ALU.mult,
                    [XB[dc][tc], rsB, lB, cB], [tmB[dc]])
                if dc % 4 == 3:
                    yield
            tt("pool", dst_all(tc), TM[:], bcl(SH, 512), ALU.add, list(tmB) + [lB, cB],
               [dstB(dc, tc) for dc in range(8)])
            if post is not None:
                for dc in range(8):
                    post(dc, tc)
            yield

        run_window((chain(tc) for tc in range(4)), 2)

    def norm_locals():
        out = []
        for _ in range(2):
            SQ = sb("SQ", [128, 8, 512], BF16)
            RS = sb("RS", [128, 512], F32)
            TM = sb("TM", [128, 8, 512], F32)
            R0 = sb("R0", [128, 512], F32)
            out.append((SQ, Buf("sq"), RS, Buf("rs"), TM, [Buf("tm%d" % i) for i in range(8)], R0, Buf("r0")))
        return out

    def mlp(l, HT, HB):
        H1 = [sb("H1a", [128, 4, 512], BF16), sb("H1b", [128, 4, 512], BF16)]
        H1B = [[Buf("h1") for _ in range(4)] for _ in range(2)]
        RL = [sb("RLa", [128, 512], BF16), sb("RLb", [128, 512], BF16)]
        RLB = [Buf("rl"), Buf("rl")]
        p1, _ = wplan[("w1", l)]
        p2, _ = wplan[("w2", l)]
        wcache = {}
        cnt = {"rl": 0}

        def weights(g):
            if g not in wcache:
                w1s = [wload(p1 + g * 2 + i) for i in range(2)]
                w2s = [wload(p2 + g * 2 + i) for i in range(2)]
                wcache[g] = (w1s, w2s)
            return wcache[g]

        def ph1(k):
            g, tt_ = k // 4, k % 4
            w1s, _ = weights(g)
            cs = slice(tt_ * 512, (tt_ + 1) * 512)
            hb = k % 2
            for ft in range(4):
                wt, wb = w1s[ft // 2]
                wv = wt[:].rearrange("p (k c) -> p k c", k=8)
                pt, pb = psum((0, 1, 2, 3))
                for kc in range(8):
                    mm(pt[:], wv[:, kc, (ft % 2) * 128:(ft % 2 + 1) * 128], HT[:, kc, cs], kc == 0, kc == 7,
                       [wb] + HB(kc, tt_), [pb])
                rb = cnt["rl"] % 2
                cnt["rl"] += 1
                act(RL[rb][:], pt[:], AF.Relu, [pb], [RLB[rb]])
                tt("pool", H1[hb][:, ft, :], RL[rb][:], RL[rb][:], ALU.mult, [RLB[rb]], [H1B[hb][ft]])

        def ph2(k):
            g, tt_ = k // 4, k % 4
            _, w2s = weights(g)
            cs = slice(tt_ * 512, (tt_ + 1) * 512)
            hb = k % 2
            for dt in range(8):
                wt, wb = w2s[dt // 4]
                wv = wt[:].rearrange("p (k c) -> p k c", k=4)
                pt, pb = psum((4, 5, 6))
                for fc in range(4):
                    mm(pt[:], wv[:, fc, (dt % 4) * 128:(dt % 4 + 1) * 128], H1[hb][:, fc, :], fc == 0, fc == 3,
                       [wb, H1B[hb][fc]], [pb])
                stt(XT[:, dt, cs], pt[:], MOD[:, l, 40 + dt:41 + dt], XT[:, dt, cs], ALU.mult, ALU.add,
                    [pb, modB[l], XB[dt][tt_]], [XB[dt][tt_]])

        ph1(0)
        for k in range(32):
            last_in_group = k % 4 == 3
            if not last_in_group:
                ph1(k + 1)
            ph2(k)
            if last_in_group:
                g = k // 4
                if l + 1 < DEPTH:
                    for pi in range(3 * g, 3 * g + 3):
                        adaln_piece(l + 1, pi)
                if k + 1 < 32:
                    ph1(k + 1)
        if l + 1 < DEPTH:
            adaln_finish(l + 1)

    def out_proj(l, key, Z, ZB):
        p0, _ = wplan[key]
        for pi in range(4):
            wt, wb = wload(p0 + pi)
            wv = wt[:].rearrange("p (k c) -> p k c", k=8)
            for d2 in range(2):
                dt = pi * 2 + d2
                for tc in range(4):
                    cs = slice(tc * 512, (tc + 1) * 512)
                    pt, pb = psum((0, 1, 2, 3))
                    for zc in range(8):
                        mm(pt[:], wv[:, zc, d2 * 128:(d2 + 1) * 128], Z[:, zc, cs], zc == 0, zc == 7,
                           [wb] + ZB(zc, tc), [pb])
                    stt(XT[:, dt, cs], pt[:], MOD[:, l, 16 + dt:17 + dt], XT[:, dt, cs], ALU.mult, ALU.add,
                        [pb, modB[l], XB[dt][tc]], [XB[dt][tc]])

    def rope_to(dst, src, tc, loc, rB, wB):
        T1, t1B, T2, t2B = loc
        pt, pb = psum((0, 1, 2, 3))
        mm(pt[:], cc("Rm"), src, True, True, list(rB) + [cB], [pb])
        tt("pool", T1[:], src, cc("cos", tc * 512, (tc + 1) * 512), ALU.mult, list(rB) + [cB], [t1B])
        tt("dve", T2[:], pt[:], cc("sin", tc * 512, (tc + 1) * 512), ALU.mult, [pb, cB], [t2B])
        tt("dve", dst, T1[:], T2[:], ALU.add, [t1B, t2B], wB)

    def small_locals(nsets=2, rope=True):
        sets = []
        for _ in range(nsets):
            d = {}
            d["SQh"] = sb("SQh", [128, 512], BF16); d["sqhB"] = Buf()
            d["RSh"] = sb("RSh", [128, 512], F32); d["rshB"] = Buf()
            d["R0h"] = sb("R0h", [128, 512], F32); d["r0hB"] = Buf()
            if rope:
                d["QN"] = sb("QN", [128, 512], F32); d["qnB"] = Buf()
                d["rope"] = (sb("T1", [128, 512], F32), Buf(), sb("T2", [128, 512], F32), Buf())
            sets.append(d)
        return sets

    def fm_proj(wv, wb, ft, HT, HB, tc, dstfn):
        cs = slice(tc * 512, (tc + 1) * 512)
        pt, pb = psum((0, 1, 2, 3))
        for kc in range(8):
            mm(pt[:], wv[:, kc, ft * 128:(ft + 1) * 128], HT[:, kc, cs], kc == 0, kc == 7, [wb] + HB(kc, tc), [pb])
        dstfn(pt, pb)

    def odd_inproj(l, i, HT, HB):
        p0, _ = wplan[("odI", i)]
        sls = small_locals()
        RAW = sb("RAW", [128, 2, T], F32)
        rawB = [[Buf() for _ in range(4)] for _ in range(2)]
        QF = sb("QF", [128, T], BF16)
        qfB = [Buf() for _ in range(4)]
        QF2 = [QF, sb("QF2", [128, T], BF16)]
        qfB2 = [qfB, [Buf() for _ in range(4)]]
        wq = {}

        def qproj(h):
            pi, ft = h // 2, h % 2
            if ft == 0:
                wt, wb = wload(p0 + pi)
                wq[pi] = (wt[:].rearrange("p (k c) -> p k c", k=8), wb)
            wv, wb = wq[pi]
            sl_ = h % 2
            for tc in range(4):
                cs = slice(tc * 512, (tc + 1) * 512)
                fm_proj(wv, wb, ft, HT, HB, tc,
                        lambda pt, pb, tc=tc, cs=cs: act(RAW[:, sl_, cs], pt[:], AF.Copy, [pb], [rawB[sl_][tc]]))

        def qchain(h, tc):
            sl_ = h % 2
            sl = sls[tc % 2]
            cs = slice(tc * 512, (tc + 1) * 512)
            act(sl["SQh"][:], RAW[:, sl_, cs], AF.Square, [rawB[sl_][tc]], [sl["sqhB"]])
            yield
            pt, pb = psum((4, 5, 6))
            mm(pt[:], onesb[:], sl["SQh"][:], True, True, [sl["sqhB"], cB], [pb])
            yield
            act(sl["R0h"][:], pt[:], AF.Ln, [pb, cB], [sl["r0hB"]], scale=1.0 / 128, bias=epsc[:, 0:1])
            yield
            act(sl["RSh"][:], sl["R0h"][:], AF.Exp, [sl["r0hB"]], [sl["rshB"]], scale=-0.5)
            yield
            stt(sl["QN"][:], RAW[:, sl_, cs], pc(("cqg", i)), sl["RSh"][:], ALU.mult, ALU.mult,
                [rawB[sl_][tc], sl["rshB"], cB], [sl["qnB"]])
            yield
            T1, t1B, T2, t2B = sl["rope"]
            pt2, pb2 = psum((0, 1, 2, 3))
            mm(pt2[:], cc("Rm"), sl["QN"][:], True, True, [sl["qnB"], cB], [pb2])
            tt("pool", T1[:], sl["QN"][:], cc("cos", tc * 512, (tc + 1) * 512), ALU.mult, [sl["qnB"], cB], [t1B])
            yield
            tt("dve", T2[:], pt2[:], cc("sin", tc * 512, (tc + 1) * 512), ALU.mult, [pb2, cB], [t2B])
            yield
            tt("dve", QF2[sl_][:, cs], T1[:], T2[:], ALU.add, [t1B, t2B], [qfB2[sl_][tc]])
            yield

        def qpost(h):
            sl_ = h % 2
            run_window((qchain(h, tc) for tc in range(4)), 2)
            P.dma("sp", out=Q_d[h], in_=QF2[sl_][:], r=qfB2[sl_], w=[Buf()])

        qproj(0)
        for h in range(8):
            if h + 1 < 8:
                qproj(h + 1)
            qpost(h)
        wk, wkb = wload(p0 + 4)
        wvv, wvb = wload(p0 + 5)
        wkv = wk[:].rearrange("p (k c) -> p k c", k=8)
        wvv_ = wvv[:].rearrange("p (k c) -> p k c", k=8)
        KN = [sb("KN", [128, 2, 128], F32) for _ in range(2)]
        knB = [Buf(), Buf()]
        VF = [sb("VF", [128, 256], F32) for _ in range(2)]
        vfB = [Buf(), Buf()]
        JK = [sb("JK", [128, 128], BF16) for _ in range(2)]
        jkB = [Buf(), Buf()]
        SSK = [sb("SSK", [128, 2], F32) for _ in range(2)]
        SSK2 = [sb("SSK2", [128, 2], F32) for _ in range(2)]
        RK = [sb("RK", [128, 2], F32) for _ in range(2)]
        skB = [Buf(), Buf()]

        def kv_tile(n):
            b = n % 2
            rs = slice(n * 128, (n + 1) * 128)
            pt, pb = psum((0, 1, 2, 3))
            for kc in range(8):
                mm(pt[:, 0:256], HT[:, kc, rs], wkv[:, kc, :], kc == 0, kc == 7, [wkb] + HB(kc, n // 4), [pb])
            for kc in range(8):
                mm(pt[:, 256:512], HT[:, kc, rs], wvv_[:, kc, :], kc == 0, kc == 7, [wvb] + HB(kc, n // 4), [pb])
            yield
            for kv in range(2):
                act(JK[b][:], pt[:, kv * 128:(kv + 1) * 128], AF.Square, [pb], [jkB[b], skB[b]],
                    accum_out=SSK[b][:, kv:kv + 1])
            act(VF[b][:], pt[:, 256:512], AF.Copy, [pb], [vfB[b]])
            yield
            ts("dve", SSK2[b][:], SSK[b][:], 1.0 / 128, ALU.mult, [skB[b]], [skB[b]], s2=EPS, op1=ALU.add)
            P.dma("sp", out=ocv_d[i, rs, :], in_=VF[b][:], r=[vfB[b]], w=[Buf()])
            P.dma("pool", out=V_d[rs, :], in_=VF[b][:], r=[vfB[b]], w=[Buf()])
            yield
            act(SSK2[b][:], SSK2[b][:], AF.Ln, [skB[b]], [skB[b]])
            yield
            act(RK[b][:], SSK2[b][:], AF.Exp, [skB[b]], [skB[b]], scale=-0.5)
            yield
            for kv in range(2):
                stt(KN[b][:, kv, :], pt[:, kv * 128:(kv + 1) * 128], RK[b][:, kv:kv + 1], pc(("ckgb", i)),
                    ALU.mult, ALU.mult, [pb, skB[b], cB], [knB[b]])
            yield
            P.dma("sp", out=ock_d[i, rs, :], in_=KN[b][:].rearrange("p a b -> p (a b)"), r=[knB[b]], w=[Buf()])
            pt2, pb2 = psum((4, 5, 6))
            for kv in range(2):
                tr(pt2[:, kv * 128:(kv + 1) * 128], KN[b][:, kv, :], cc("ident"), [knB[b], cB], [pb2])
            yield
            act(RAW[:, :, rs], pt2[:, 0:256].rearrange("p (a b) -> p a b", a=2), AF.Copy, [pb2],
                [rawB[0][n // 4], rawB[1][n // 4]])
            yield

        run_window((kv_tile(n) for n in range(NT)), 2)
        for kv in range(2):
            for tc in range(4):
                cs = slice(tc * 512, (tc + 1) * 512)
                rope_to(QF2[kv][:, cs], RAW[:, kv, cs], tc, sls[tc % 2]["rope"], [rawB[kv][tc]], [qfB2[kv][tc]])
            P.dma("sp", out=K_d[kv], in_=QF2[kv][:], r=qfB2[kv], w=[Buf()])

    def attn_loads(ck_d, cv_d, i, nq):
        QZ = sb("QZ", [128, nq, T], BF16)
        qzB = [[Buf() for _ in range(NT)] for _ in range(2)]
        KT = sb("KT", [128, 2, T], BF16); ktB = Buf()
        V = sb("V", [128, NT, 256], BF16); vB = Buf()
        CK = sb("CK", [128, 2, 512], BF16)
        CV = sb("CV", [128, 4, 256], BF16)
        g = nq // 2
        for h in range(nq):
            P.dma("sp", out=QZ[:, h, :], in_=Q_d[h], w=qzB[h // g])
        for kv in range(2):
            P.dma("sp", out=KT[:, kv, :], in_=K_d[kv], w=[ktB])
        P.dma("sp", out=V[:], in_=V_d.rearrange("(n p) c -> p n c", p=128), w=[vB])
        P.dma("pool", out=CK[:], in_=ck_d[i].rearrange("k d t -> d k t"), w=[ktB])
        P.dma("pool", out=CV[:], in_=cv_d[i].rearrange("(m p) c -> p m c", p=128), w=[vB])
        return QZ, qzB, KT, ktB, V, vB, CK, CV

    def odd_attn(l, i):
        QZ, qzB, KT, ktB, V, vB, CK, CV = attn_loads(ckc_d, cvc_d, i, 8)
        NPT = 6
        PT = [sb("PT", [128, 512], BF16) for _ in range(NPT)]
        ptB = [Buf() for _ in range(NPT)]
        RC = [sb("RC", [128, 512], F32) for _ in range(2)]
        rcB = [Buf(), Buf()]
        cnt = {"it": 0}

        def unit(kv, n):
            ns = slice(n * 128, (n + 1) * 128)
            rhs = QZ[:, kv * 4:(kv + 1) * 4, ns]
            ot, otb = psum((4, 5))
            su, sub = psum((6, 7))

            def qk(m):
                st, stb = psum((0, 1, 2, 3))
                lhsT = CK[:, kv, m * 128:(m + 1) * 128] if m < 4 else KT[:, kv, (m - 4) * 128:(m - 3) * 128]
                mm(v4(st[:]), lhsT, rhs, True, True, [ktB, qzB[kv][n]], [stb])
                return st, stb

            nxt = qk(0)
            yield
            for m in range(20):
                st, stb = nxt
                if m + 1 < 20:
                    nxt = qk(m + 1)
                pi = cnt["it"] % NPT
                cnt["it"] += 1
                act(PT[pi][:], st[:], AF.Exp, [stb, cB], [ptB[pi]], scale=SCALE,
                    bias=cc("biasC", n * 20 + m, n * 20 + m + 1))
                vl = CV[:, m, kv * 128:(kv + 1) * 128] if m < 4 else V[:, m - 4, kv * 128:(kv + 1) * 128]
                mm(ot[:], vl, PT[pi][:], m == 0, m == 19, [ptB[pi], vB], [otb])
                mm(su[:], onesb[:], PT[pi][:], m == 0, m == 19, [ptB[pi], cB], [sub])
                yield
            rb = (kv * NT + n) % 2
            recip(RC[rb][:], su[:], [sub], [rcB[rb]])
            tt("dve", QZ[:, kv * 4:(kv + 1) * 4, ns], v4(ot[:]), v4(RC[rb][:]), ALU.mult,
               [otb, rcB[rb]], [qzB[kv][n]])
            yield

        run_window((unit(kv, n) for kv in range(2) for n in range(NT)), 2)
        out_proj(l, ("odO", i), QZ, lambda zc, tc: [qzB[zc // 4][n] for n in range(tc * 4, tc * 4 + 4)])

    def even_inprojA(l, i, HT, HB):
        p0, _ = wplan[("evA", i)]
        sls = small_locals(2, rope=False)
        RAWs = [sb("RAWc", [128, T + 2], F32) for _ in range(2)]
        rawBs = [Buf(), Buf()]
        CVs = [sb("CVt", [128, T], F32) for _ in range(2)]
        cvBs = [[Buf() for _ in range(4)] for _ in range(2)]
        FINs = [sb("FIN", [128, T], BF16) for _ in range(2)]
        finBs = [[Buf() for _ in range(4)] for _ in range(2)]
        NC0 = sb("NC0", [128, 2, 2], F32); ncB = [Buf(), Buf()]
        for k in range(2):
            P.op("dve", lambda e, k=k: e.memset(RAWs[k][:, 0:1], 0.0), [], [rawBs[k]])
            P.op("dve", lambda e, k=k: e.memset(RAWs[k][:, T + 1:T + 2], 0.0), [], [rawBs[k]])
        dsts = [Aq_d, Ak_d, Av_d, Ag_d]
        wcache = {}

        def proj(j):
            pi, ft = j // 2, j % 2
            kind = pi // 2
            if ft == 0:
                wt, wb = wload(p0 + pi)
                wcache[pi] = (wt[:].rearrange("p (k c) -> p k c", k=8), wb)
            wv, wb = wcache[pi]
            k = j % 2
            for tc in range(4):
                cs = slice(tc * 512, (tc + 1) * 512)
                if kind == 3:
                    fm_proj(wv, wb, ft, HT, HB, tc,
                            lambda pt, pb, tc=tc, cs=cs: act(FINs[k][:, cs], pt[:], AF.Silu, [pb], [finBs[k][tc]]))
                else:
                    fm_proj(wv, wb, ft, HT, HB, tc,
                            lambda pt, pb, tc=tc: cp("dve", RAWs[k][:, 1 + tc * 512:1 + (tc + 1) * 512], pt[:],
                                                     [pb], [rawBs[k]]))

        def post(j):
            pi, ft = j // 2, j % 2
            kind = pi // 2
            h = (pi % 2) * 2 + ft
            k = j % 2
            RAW, rawB, CVt, cvB, FIN, finB = RAWs[k], rawBs[k], CVs[k], cvBs[k], FINs[k], finBs[k]
            if kind == 3:
                P.dma("sp", out=Ag_d[h], in_=FIN[:], r=finB, w=[Buf()])
                return
            ct = kind * 4 + h
            w0 = pc(("conv", i), ct * 3 + 0, ct * 3 + 1)
            w1 = pc(("conv", i), ct * 3 + 1, ct * 3 + 2)
            w2 = pc(("conv", i), ct * 3 + 2, ct * 3 + 3)
            ts("dve", CVt[:], RAW[:, 1:T + 1], w1, ALU.mult, [rawB, cB], cvB)
            stt(CVt[:], RAW[:, 0:T], w0, CVt[:], ALU.mult, ALU.add, [rawB, cB] + cvB, cvB)
            stt(CVt[:], RAW[:, 2:T + 2], w2, CVt[:], ALU.mult, ALU.add, [rawB, cB] + cvB, cvB)
            ts("dve", NC0[:, k, 0:1], w0, cc("nflag"), ALU.mult, [cB], [ncB[k]], s2=-1.0, op1=ALU.mult)
            ts("dve", NC0[:, k, 1:2], w2, cc("nflag"), ALU.mult, [cB], [ncB[k]], s2=-1.0, op1=ALU.mult)
            stt(CVt[:, 256:T:256], RAW[:, 256:T:256], NC0[:, k, 0:1], CVt[:, 256:T:256], ALU.mult, ALU.add,
                [rawB, ncB[k]] + cvB, cvB)
            stt(CVt[:, 255:T - 1:256], RAW[:, 257:T + 1:256], NC0[:, k, 1:2], CVt[:, 255:T - 1:256], ALU.mult, ALU.add,
                [rawB, ncB[k]] + cvB, cvB)
            if kind == 2:
                act(FIN[:], CVt[:], AF.Silu, cvB, finB)
            else:
                def chain(tc):
                    sl = sls[tc % 2]
                    cs = slice(tc * 512, (tc + 1) * 512)
                    act(CVt[:, cs], CVt[:, cs], AF.Silu, [cvB[tc]], [cvB[tc]])
                    yield
                    act(sl["SQh"][:], CVt[:, cs], AF.Square, [cvB[tc]], [sl["sqhB"]])
                    yield
                    pt, pb = psum((4, 5, 6))
                    mm(pt[:], onesb[:], sl["SQh"][:], True, True, [sl["sqhB"], cB], [pb])
                    yield
                    act(sl["R0h"][:], pt[:], AF.Ln, [pb, cB], [sl["r0hB"]], scale=1.0, bias=epsc[:, 0:1])
                    yield
                    act(sl["RSh"][:], sl["R0h"][:], AF.Exp, [sl["r0hB"]], [sl["rshB"]], scale=-0.5)
                    yield
                    stt(FIN[:, cs], CVt[:, cs], SCALE if kind == 0 else 1.0, sl["RSh"][:], ALU.mult, ALU.mult,
                        [cvB[tc], sl["rshB"]], [finB[tc]])
                    yield

                run_window((chain(tc) for tc in range(4)), 2)
            P.dma("sp", out=dsts[kind][h], in_=FIN[:], r=finB, w=[Buf()])

        proj(0)
        for j in range(16):
            if j + 1 < 16:
                proj(j + 1)
            post(j)
        BGR = sb("BGR", [128, NT, 16], F32); bgrB = Buf()
        E1 = sb("bE1", [128, NT, 8], F32); XA = sb("bXA", [128, NT, 8], F32)
        AB = sb("bAB", [128, NT, 8], F32); L2 = sb("bL2", [128, NT, 8], F32)
        for n in range(NT):
            rs = slice(n * 128, (n + 1) * 128)
            pt, pb = psum((4, 5, 6))
            for kc in range(8):
                mm(pt[:, 0:16], HT[:, kc, rs], WBG[:, i, kc * 16:(kc + 1) * 16], kc == 0, kc == 7,
                   [cB] + HB(kc, n // 4), [pb])
            cp("dve", BGR[:, n, :], pt[:, 0:16], [pb], [bgrB])
        act(E1[:], BGR[:, :, 0:8], AF.Exp, [bgrB], [bgrB], scale=-1.0)
        ts("dve", E1[:], E1[:], 1.0, ALU.add, [bgrB], [bgrB])
        recip(BETA[:], E1[:], [bgrB], [bgB])
        tt("dve", XA[:], BGR[:, :, 8:16], bcm(pc(("dtb", i)), NT), ALU.add, [bgrB, cB], [bgrB])
        stt(AB[:], XA[:], -1.0, XA[:], ALU.mult, ALU.max, [bgrB], [bgrB])
        act(AB[:], AB[:], AF.Exp, [bgrB], [bgrB], scale=-1.0)
        act(L2[:], AB[:], AF.Ln, [bgrB], [bgrB], bias=1.0)
        stt(XA[:], XA[:], 0.0, L2[:], ALU.max, ALU.add, [bgrB], [bgrB])
        tt("dve", GG[:], XA[:], bcm(NEA[:, i * 8:(i + 1) * 8], NT), ALU.mult, [bgrB, cB], [bgB])

    def even_inprojB(l, i, HT, HB):
        p0, _ = wplan[("evB", i)]
        sls = small_locals()
        RAWs = [sb("RAWb", [128, T], F32) for _ in range(2)]
        rawBs = [[Buf() for _ in range(4)] for _ in range(2)]
        QFs = [sb("QFb", [128, T], BF16) for _ in range(2)]
        qfBs = [[Buf() for _ in range(4)] for _ in range(2)]
        wcache = {}

        def proj(j):
            pi, ft = j // 2, j % 2
            if ft == 0:
                wt, wb = wload(p0 + pi)
                wcache[pi] = (wt[:].rearrange("p (k c) -> p k c", k=8), wb)
            wv, wb = wcache[pi]
            RAW, rawB = RAWs[j % 2], rawBs[j % 2]
            for tc in range(4):
                cs = slice(tc * 512, (tc + 1) * 512)
                fm_proj(wv, wb, ft, HT, HB, tc,
                        lambda pt, pb, tc=tc, cs=cs: act(RAW[:, cs], pt[:], AF.Copy, [pb], [rawB[tc]]))

        def post(j):
            pi, ft = j // 2, j % 2
            RAW, rawB = RAWs[j % 2], rawBs[j % 2]
            QF, qfB = QFs[j % 2], qfBs[j % 2]
            if pi == 2:
                P.dma("sp", out=obk_d[i, ft], in_=RAW[:], r=rawB, w=[Buf()])
            def rchain(tc):
                cs = slice(tc * 512, (tc + 1) * 512)
                T1, t1B, T2, t2B = sls[tc % 2]["rope"]
                pt, pb = psum((0, 1, 2, 3))
                mm(pt[:], cc("Rm"), RAW[:, cs], True, True, [rawB[tc], cB], [pb])
                tt("pool", T1[:], RAW[:, cs], cc("cos", tc * 512, (tc + 1) * 512), ALU.mult, [rawB[tc], cB], [t1B])
                yield
                tt("dve", T2[:], pt[:], cc("sin", tc * 512, (tc + 1) * 512), ALU.mult, [pb, cB], [t2B])
                yield
                tt("dve", QF[:, cs], T1[:], T2[:], ALU.add, [t1B, t2B], [qfB[tc]])
                yield

            run_window((rchain(tc) for tc in range(4)), 2)
            if pi < 2:
                P.dma("sp", out=Q_d[pi * 2 + ft], in_=QF[:], r=qfB, w=[Buf()])
            else:
                P.dma("sp", out=K_d[ft], in_=QF[:], r=qfB, w=[Buf()])

        proj(0)
        for j in range(6):
            if j + 1 < 6:
                proj(j + 1)
            post(j)
        wt, wb = wload(p0 + 3)
        wv = wt[:].rearrange("p (k c) -> p k c", k=8)
        VF = [sb("VFb", [128, 256], F32) for _ in range(2)]
        vfB = [Buf(), Buf()]
        for n in range(NT):
            b = n % 2
            rs = slice(n * 128, (n + 1) * 128)
            pt, pb = psum((4, 5, 6))
            for kc in range(8):
                mm(pt[:, 0:256], HT[:, kc, rs], wv[:, kc, :], kc == 0, kc == 7, [wb] + HB(kc, n // 4), [pb])
            act(VF[b][:], pt[:, 0:256], AF.Copy, [pb], [vfB[b]])
            P.dma("sp", out=obv_d[i, rs, :], in_=VF[b][:], r=[vfB[b]], w=[Buf()])
            P.dma("pool", out=V_d[rs, :], in_=VF[b][:], r=[vfB[b]], w=[Buf()])

    def even_attnB(l, i):
        QZ, qzB, KT, ktB, V, vB, CK, CV = attn_loads(ckb_d, cvb_d, i, 4)
        PT = [sb("PTb", [128, 256], BF16) for _ in range(6)]
        ptB = [Buf() for _ in range(6)]
        DN = [sb("DN", [128, 256], F32) for _ in range(2)]
        RC = [sb("RCb", [128, 256], F32) for _ in range(2)]
        rcB = [Buf(), Buf()]
        MPT = sb("MPT", [128, 128], BF16)
        MNT = sb("MNT", [128, 128], BF16)
        mB = Buf()
        cp("dve", MPT[:], cc("mPT"), [cB], [mB])
        cp("dve", MNT[:], cc("mNT"), [cB], [mB])
        cnt = {"it": 0}

        def unit(kv, n):
            ns = slice(n * 128, (n + 1) * 128)
            rhs = QZ[:, kv * 2:(kv + 1) * 2, ns]
            blocks = [(m, "c", m) for m in range(4)]
            if n > 0:
                blocks.append((4, "p", n - 1))
            blocks.append((5, "s", n))
            if n < NT - 1:
                blocks.append((6, "n", n + 1))
            ot, otb = psum((4, 5))
            su, sub = psum((6, 7))

            def qk(bl):
                slot, kind, idx = bl
                st, stb = psum((0, 1, 2, 3))
                lhsT = CK[:, kv, idx * 128:(idx + 1) * 128] if kind == "c" else KT[:, kv, idx * 128:(idx + 1) * 128]
                mm(v4(st[:, 0:256]), lhsT, rhs, True, True, [ktB, qzB[kv][n]], [stb])
                return st, stb

            nxt = qk(blocks[0])
            yield
            for bi, bl in enumerate(blocks):
                slot, kind, idx = bl
                st, stb = nxt
                if bi + 1 < len(blocks):
                    nxt = qk(blocks[bi + 1])
                pi = cnt["it"] % len(PT)
                cnt["it"] += 1
                act(PT[pi][:], st[:, 0:256], AF.Exp, [stb, cB], [ptB[pi]], scale=SCALE,
                    bias=cc("biasB", n * 7 + slot, n * 7 + slot + 1))
                if kind in ("p", "n"):
                    mk = MPT if kind == "p" else MNT
                    tt("pool", v4(PT[pi][:]), v4(PT[pi][:]), bcm(mk[:], 2), ALU.mult, [ptB[pi], mB], [ptB[pi]])
                vl = CV[:, idx, kv * 128:(kv + 1) * 128] if kind == "c" else V[:, idx, kv * 128:(kv + 1) * 128]
                first, last = bi == 0, bi == len(blocks) - 1
                mm(ot[:, 0:256], vl, PT[pi][:], first, last, [ptB[pi], vB], [otb])
                mm(su[:, 0:256], onesb[:], PT[pi][:], first, last, [ptB[pi], cB], [sub])
                yield
            rb = (kv * NT + n) % 2
            for g in range(2):
                hh = i * 4 + kv * 2 + g
                ts("dve", DN[rb][:, g * 128:(g + 1) * 128], su[:, g * 128:(g + 1) * 128], EXS[:, hh:hh + 1], ALU.add,
                   [sub, cB], [rcB[rb]])
            recip(RC[rb][:], DN[rb][:], [rcB[rb]], [rcB[rb]])
            tt("dve", QZ[:, kv * 2:(kv + 1) * 2, ns], v4(ot[:, 0:256]), v4(RC[rb][:]), ALU.mult,
               [otb, rcB[rb]], [qzB[kv][n]])
            yield

        run_window((unit(kv, n) for kv in range(2) for n in range(NT)), 2)
        for h in range(4):
            P.dma("sp", out=Z_d[4 + h], in_=QZ[:, h, :], r=qzB[h // 2], w=[Buf()])

    def gdn_pass(l, i, dr, Ao_dst):
        order = list(range(NT)) if dr == 0 else list(range(NT - 1, -1, -1))
        tri = cc("triF") if dr == 0 else cc("triB")
        Mi = cc("MiF") if dr == 0 else cc("MiB")
        Ms = cc("MsF") if dr == 0 else cc("MsB")
        last = 127 if dr == 0 else 0
        d4 = slice(dr * 4, dr * 4 + 4)

        def t4(name, dt, nb=1):
            ts_ = [sb(name, [128, 4, 128], dt) for _ in range(nb)]
            return ts_, [Buf() for _ in range(nb)]

        QTt, qtB = t4("QTt", BF16, 2)
        KTt, ktB = t4("KTt", BF16, 2)
        VTt, vtB = t4("VTt", BF16, 2)
        OST, ostB = t4("OST", F32, 1)
        KVTM = sb("KVTM", [128, 2, 4, 128], BF16); kvtmB = Buf()
        BA, baB = t4("BA", F32)
        BB, bbB = t4("BB", F32)
        EGB, egbB = t4("EGB", F32)
        QKTb, qktbB = t4("QKTb", BF16)
        Pc, pcB = t4("Pc", CH_DT, 2)
        Qc, qcB = t4("Qc", CH_DT, 2)
        Gc, gcB = t4("Gc", CH_DT, 2)
        TB, tbB = t4("TB", BF16)
        TBE, tbeB = t4("TBE", BF16)
        U, uB = t4("U", F32)
        WT, wtB = t4("WT", BF16)
        QET, qetB = t4("QET", BF16)
        VN, vnB = t4("VN", BF16)
        VN2, vn2B = t4("VN2", BF16)
        S, sB = t4("S", F32)
        Sb, sbB = t4("Sb", BF16)
        SO, soB = t4("SO", F32, 1)
        SM = sb("SM", [128, 8, 4], F32); smB = Buf()
        GCOL, NBc, EGC, BEc, DC0, DCOL = (SM[:, k, :] for k in range(6))
        identf = cc("ident")
        onesf = cc("ones")
        A, aB = BA[0], baB[0]
        Bm, bB = BB[0], bbB[0]

        P.dma("sp", out=S[0][:], in_=s0_d[i, dr].rearrange("h k v -> k h v"), w=[sB[0]])
        cp("dve", Sb[0][:], S[0][:], [sB[0]], [sbB[0]])

        def loads(step):
            n = order[step]
            b = step % 2
            ns = slice(n * 128, (n + 1) * 128)
            P.dma("sp", out=QTt[b][:], in_=Aq_d[:, :, ns].rearrange("h d t -> d h t"), w=[qtB[b]])
            P.dma("sp", out=KTt[b][:], in_=Ak_d[:, :, ns].rearrange("h d t -> d h t"), w=[ktB[b]])
            P.dma("sp", out=VTt[b][:], in_=Av_d[:, :, ns].rearrange("h d t -> d h t"), w=[vtB[b]])

        def hmm(pt, pb, lf, rf, r, start=True, stop=True):
            for h in range(4):
                mm(pt[:, h * 128:(h + 1) * 128], lf(h), rf(h), start, stop, r, [pb])

        def hmm_t(pt_, pb_, src, sB_):
            for h in range(4):
                tr(pt_[:, h * 128:(h + 1) * 128], src[:, h, :], identf, [sB_, cB], [pb_])

        loads(0)
        yield
        for step in range(NT):
            n = order[step]
            b = step % 2
            ns = slice(n * 128, (n + 1) * 128)
            if step + 1 < NT:
                loads(step + 1)
            q_, k_, v_ = QTt[b], KTt[b], VTt[b]
            pt, pb = psum()
            ptb = pt[:].bitcast(BF16)
            for h in range(4):
                tr(ptb[:, h * 128:(h + 1) * 128], k_[:, h, :], identb[:], [ktB[b], cB], [pb])
            for h in range(4):
                tr(ptb[:, 512 + h * 128:512 + (h + 1) * 128], v_[:, h, :], identb[:], [vtB[b], cB], [pb])
            act(KVTM[:].rearrange("p a b c -> p (a b c)"), ptb[:, 0:1024], AF.Copy, [pb], [kvtmB])
            KTM = KVTM[:, 0]
            VTM = KVTM[:, 1]
            pg, pgb_ = psum()
            mm(pg[:, 0:4], tri, GG[:, n, d4], True, True, [cB, bgB], [pgb_])
            cp("dve", GCOL, pg[:, 0:4], [pgb_], [smB])
            tt("dve", A[:], bcm(tri, 4), bcl(GG[:, n, d4], 128), ALU.mult, [cB, bgB], [aB])
            yield
            pgc, pgcb = psum()
            mm(pgc[:], onesf, A[:].rearrange("p a b -> p (a b)"), True, True, [cB, aB], [pgcb])
            for h in range(4):
                ts("dve", A[:, h, :], pgc[:, h * 128:(h + 1) * 128], GCOL[:, h:h + 1], ALU.subtract, [pgcb, smB], [aB],
                   s2=0.0, op1=ALU.max)
            act(A[:], A[:], AF.Exp, [aB], [aB], scale=-1.0)
            act(EGB[0][:].rearrange("p a b -> p (a b)"), pgc[:], AF.Exp, [pgcb], [egbB[0]])
            tt("dve", DC0, v4(pgc[:])[:, :, last], GCOL, ALU.subtract, [pgcb, smB], [smB])
            act(DCOL, DC0, AF.Exp, [smB], [smB])
            act(EGC, GCOL, AF.Exp, [smB], [smB])
            ts("dve", NBc, BETA[:, n, d4], -1.0, ALU.mult, [bgB], [smB])
            tt("dve", BEc, BETA[:, n, d4], EGC, ALU.mult, [bgB, smB], [smB])
            tt("dve", Bm[:], bcm(Ms, 4), bcl(NBc, 128), ALU.mult, [cB, smB], [bB])
            yield
            pkk, pkkb = psum()
            hmm(pkk, pkkb, lambda h: k_[:, h, :], lambda h: k_[:, h, :], [ktB[b]])
            pqk, pqkb = psum()
            hmm(pqk, pqkb, lambda h: q_[:, h, :], lambda h: k_[:, h, :], [ktB[b], qtB[b]])
            stt(Bm[:], A[:], 1.0, Bm[:], ALU.min, ALU.mult, [aB, bB], [bB])
            stt(A[:], A[:], 1.0, bcm(Mi, 4), ALU.min, ALU.mult, [aB, cB], [aB])
            tt("dve", Bm[:], v4(pkk[:]), Bm[:], ALU.mult, [pkkb, bB], [bB])
            tt("dve", A[:], v4(pqk[:]), A[:], ALU.mult, [pqkb, aB], [aB])
            act(Qc[0][:], Bm[:], AF.Copy, [bB], [qcB[0]])
            yield
            pp, ppb = psum()
            hmm_t(pp, ppb, Bm, bB)
            act(Pc[0][:], v4(pp[:]), AF.Copy, [ppb], [pcB[0]])
            tt("dve", Gc[0][:], v4(pp[:]), bcm(identf, 4), ALU.add, [ppb, cB], [gcB[0]])
            pq2, pq2b = psum()
            hmm_t(pq2, pq2b, A, aB)
            act(QKTb[0][:], v4(pq2[:]), AF.Copy, [pq2b], [qktbB[0]])
            tt("pool", QET[0][:], q_[:], EGB[0][:], ALU.mult, [qtB[b], egbB[0]], [qetB[0]])
            yield
            cur = 0
            NLEV = 4
            for j in range(NLEV):
                nx = 1 - cur
                pa, pab = psum()
                hmm(pa, pab, lambda h: Pc[cur][:, h, :], lambda h: Qc[cur][:, h, :], [pcB[cur], qcB[cur]])
                act(Qc[nx][:], v4(pa[:]), AF.Copy, [pab], [qcB[nx]])
                if j < NLEV - 1:
                    pb2, pb2b = psum()
                    hmm(pb2, pb2b, lambda h: Qc[cur][:, h, :], lambda h: Pc[cur][:, h, :], [pcB[cur], qcB[cur]])
                    cp("dve", Pc[nx][:], v4(pb2[:]), [pb2b], [pcB[nx]])
                yield
                pc3, pc3b = psum()
                hmm(pc3, pc3b, lambda h: Qc[nx][:, h, :], lambda h: Gc[cur][:, h, :], [qcB[nx], gcB[cur]])
                tt("dve", Gc[nx][:], Gc[cur][:], v4(pc3[:]), ALU.add, [gcB[cur], pc3b], [gcB[nx]])
                cur = nx
                yield
            Gf, gfB = Gc[cur], gcB[cur]
            pmx, pmxb = psum()
            hmm(pmx, pmxb, lambda h: Bm[:, h, :], lambda h: Gf[:, h, :], [bB, gfB])
            Rt, rB = Pc[0], pcB[0]
            tt("dve", Rt[:], v4(pmx[:]), Gf[:], ALU.subtract, [pmxb, gfB], [rB])
            tt("pool", Rt[:], Rt[:], bcm(identf, 4), ALU.add, [rB, cB], [rB])
            pxt, pxtb = psum()
            hmm_t(pxt, pxtb, Gf, gfB)
            act(Qc[0][:], v4(pxt[:]), AF.Copy, [pxtb], [qcB[0]])
            yield
            pxr, pxrb = psum()
            hmm(pxr, pxrb, lambda h: Qc[0][:, h, :], lambda h: Rt[:, h, :], [qcB[0], rB])
            Gn, gnB = Gc[1 - cur], gcB[1 - cur]
            tt("dve", Gn[:], Gf[:], v4(pxr[:]), ALU.add, [gfB, pxrb], [gnB])
            Gf, gfB = Gn, gnB
            yield
            tt("dve", TB[0][:], Gf[:], bcl(BETA[:, n, d4], 128), ALU.mult, [gfB, bgB], [tbB[0]])
            tt("pool", TBE[0][:], Gf[:], bcl(BEc, 128), ALU.mult, [gfB, smB], [tbeB[0]])
            pu, pub = psum()
            hmm(pu, pub, lambda h: TB[0][:, h, :], lambda h: VTM[:, h, :], [tbB[0], kvtmB])
            act(U[0][:], v4(pu[:]), AF.Copy, [pub], [uB[0]])
            pw, pwb = psum()
            hmm(pw, pwb, lambda h: KTM[:, h, :], lambda h: TBE[0][:, h, :], [tbeB[0], kvtmB])
            act(WT[0][:], v4(pw[:]), AF.Copy, [pwb], [wtB[0]])
            yield
            if step > 0 and step % 2 == 0:
                ts("dve", S[0][:], S[0][:], cc("flag"), ALU.mult, [sB[0], cB], [sB[0]])
                ts("dve", Sb[0][:], Sb[0][:], cc("flag"), ALU.mult, [sbB[0], cB], [sbB[0]])
            pws, pwsb = psum()
            hmm(pws, pwsb, lambda h: WT[0][:, h, :], lambda h: Sb[0][:, h, :], [wtB[0], sbB[0]])
            tt("dve", VN[0][:], U[0][:], v4(pws[:]), ALU.subtract, [uB[0], pwsb], [vnB[0]])
            tt("pool", VN2[0][:], VN[0][:], bcl(DCOL, 128), ALU.mult, [vnB[0], smB], [vn2B[0]])
            yield
            po, pob = psum()
            for h in range(4):
                hs = slice(h * 128, (h + 1) * 128)
                mm(po[:, hs], Sb[0][:, h, :], QET[0][:, h, :], True, False, [sbB[0], qetB[0]], [pob])
                mm(po[:, hs], VN[0][:, h, :], QKTb[0][:, h, :], False, True, [vnB[0], qktbB[0]], [pob])
            act(OST[0][:], v4(po[:]), AF.Copy, [pob], [ostB[0]])
            P.dma("sp", out=Ao_dst[:, :, ns].rearrange("h d t -> d h t"), in_=OST[0][:], r=[ostB[0]], w=[Buf()])
            pds, pdsb = psum()
            hmm(pds, pdsb, lambda h: KTM[:, h, :], lambda h: VN2[0][:, h, :], [kvtmB, vn2B[0]])
            yield
            for h in range(4):
                stt(S[0][:, h, :], S[0][:, h, :], EGB[0][:, h, last:last + 1], pds[:, h * 128:(h + 1) * 128],
                    ALU.mult, ALU.add, [sB[0], egbB[0], pdsb], [sB[0]])
            act(Sb[0][:], S[0][:], AF.Copy, [sB[0]], [sbB[0]])
            if step % 2 == 1:
                seg = n // 2
                k2 = 0
                cp("pool", SO[k2][:], S[0][:], [sB[0]], [soB[k2]])
                P.dma("sp", out=ost_d[i, seg, dr].rearrange("h k v -> k h v"), in_=SO[k2][:], r=[soB[k2]], w=[Buf()])
            yield

    def run_interleaved(gens):
        gens = list(gens)
        if SEQ_GDN:
            for g in gens:
                for _ in g:
                    pass
            return
        while gens:
            for g in list(gens):
                try:
                    next(g)
                except StopIteration:
                    gens.remove(g)

    def gdn_combine(l, i):
        def t4(name, dt, nb=1):
            ts_ = [sb(name, [128, 4, 128], dt) for _ in range(nb)]
            return ts_, [Buf() for _ in range(nb)]
        NBUF = 4
        OF, ofB = t4("OF", F32, NBUF)
        OB, obB = t4("OB", F32, NBUF)
        GT, gtB = t4("GT", BF16, NBUF)
        ZA, zaB = t4("ZA", BF16, NBUF)
        O_, oB = t4("O_", F32, NBUF)
        SQo, sqoB = t4("SQo", BF16, NBUF)
        RSo, rsoB = t4("RSo", F32, NBUF)
        R0o, r0oB = t4("R0o", F32, NBUF)

        def loads(n):
            b = n % NBUF
            ns = slice(n * 128, (n + 1) * 128)
            P.dma("sp", out=OF[b][:], in_=Aof_d[:, :, ns].rearrange("h d t -> d h t"), w=[ofB[b]])
            P.dma("sp", out=OB[b][:], in_=Ao_d[:, :, ns].rearrange("h d t -> d h t"), w=[obB[b]])
            P.dma("sp", out=GT[b][:], in_=Ag_d[:, :, ns].rearrange("h d t -> d h t"), w=[gtB[b]])

        def unit(n):
            b = n % NBUF
            ns = slice(n * 128, (n + 1) * 128)
            loads(n)
            yield
            tt("pool", O_[b][:], OF[b][:], OB[b][:], ALU.add, [ofB[b], obB[b]], [oB[b]])
            yield
            act(SQo[b][:], O_[b][:], AF.Square, [oB[b]], [sqoB[b]])
            yield
            pss, pssb = psum()
            mm(pss[:], onesb[:], SQo[b][:].rearrange("p a b -> p (a b)"), True, True, [sqoB[b], cB], [pssb])
            yield
            act(R0o[b][:].rearrange("p a b -> p (a b)"), pss[:], AF.Ln, [pssb, cB], [r0oB[b]], scale=1.0 / 128,
                bias=epsc[:, 0:1])
            yield
            act(RSo[b][:], R0o[b][:], AF.Exp, [r0oB[b]], [rsoB[b]], scale=-0.5)
            yield
            stt(O_[b][:], O_[b][:], pc(("ang", i)), RSo[b][:], ALU.mult, ALU.mult, [oB[b], rsoB[b], cB], [oB[b]])
            yield
            tt("pool", ZA[b][:], O_[b][:], GT[b][:], ALU.mult, [oB[b], gtB[b]], [zaB[b]])
            P.dma("sp", out=Z_d[0:4, :, ns].rearrange("h d t -> d h t"), in_=ZA[b][:], r=[zaB[b]], w=[Buf()])
            yield

        run_window((unit(n) for n in range(NT)), NBUF)

    def HBf(HB):
        return lambda kc, tc: [HB[kc][tc]]

    def make_HT():
        HT = sb("HT", [128, 8, T], BF16)
        HB = [[Buf() for _ in range(4)] for _ in range(8)]
        return HT, HB

    P.mark('adaln0')
    with phase():
        ring_alloc()
        adaln(0)
    for l in range(DEPTH):
        i = l // 2
        do_mixer = (l % 2 == 0 and DO_EVEN) or (l % 2 == 1 and DO_ODD)
        if do_mixer:
            with phase():
                HT, HB = make_HT()
                with phase():
                    P.mark('L%d norm1' % l)
                    norm_mod(A1[:, l, :], MOD[:, l, 0:8], lambda dc, tc: HT[:, dc, tc * 512:(tc + 1) * 512],
                             lambda dc, tc: HB[dc][tc], modB[l], norm_locals(),
                             dst_all=lambda tc: HT[:, :, tc * 512:(tc + 1) * 512])
                if l % 2 == 1:
                    with phase():
                        P.mark('L%d odd_inproj' % l)
                        ring_alloc()
                        odd_inproj(l, i, HT, HBf(HB))
                else:
                    with phase():
                        P.mark('L%d even_inprojA' % l)
                        ring_alloc()
                        even_inprojA(l, i, HT, HBf(HB))
                    with phase():
                        P.mark('L%d even_inprojB' % l)
                        ring_alloc()
                        even_inprojB(l, i, HT, HBf(HB))
            if l % 2 == 1:
                with phase():
                    P.mark('L%d odd_attn' % l)
                    ring_alloc()
                    odd_attn(l, i)
            else:
                with phase():
                    P.mark('L%d even_attnB' % l)
                    even_attnB(l, i)
                with phase():
                    P.mark('L%d gdn' % l)
                    run_interleaved([gdn_pass(l, i, 1, Ao_d), gdn_pass(l, i, 0, Aof_d)])
                with phase():
                    P.mark('L%d gdn_comb' % l)
                    gdn_combine(l, i)
                with phase():
                    P.mark('L%d even_outproj' % l)
                    ring_alloc()
                    Z = sb("Z", [128, 8, T], BF16)
                    zB = [Buf() for _ in range(8)]
                    for zc in range(8):
                        P.dma("sp", out=Z[:, zc, :], in_=Z_d[zc], w=[zB[zc]])
                    out_proj(l, ("evO", i), Z, lambda zc, tc: [zB[zc]])
        with phase():
            HT, HB = make_HT()
            with phase():
                P.mark('L%d norm2' % l)
                norm_mod(A2[:, l, :], MOD[:, l, 24:32], lambda dc, tc: HT[:, dc, tc * 512:(tc + 1) * 512],
                         lambda dc, tc: HB[dc][tc], modB[l], norm_locals(),
                         dst_all=lambda tc: HT[:, :, tc * 512:(tc + 1) * 512])
            P.mark('L%d mlp' % l)
            ring_alloc()
            mlp(l, HT, HBf(HB))
    with phase():
        P.mark('final')
        YT = sb("YT", [128, 8, 512], F32)
        yB = [Buf() for _ in range(8)]
        norm_mod(pc("fing"), zero8[:], lambda dc, tc: YT[:, dc, :], lambda dc, tc: yB[dc], cB, norm_locals(),
                 post=lambda dc, tc: P.dma("sp", out=yT_d[dc * 128:(dc + 1) * 128, tc * 512:(tc + 1) * 512],
                                           in_=YT[:, dc, :], r=[yB[dc]], w=[Buf()]),
                 dst_all=lambda tc: YT[:])
    P.barrier()
    global LAST_PROG
    LAST_PROG = P
    return nc


def _rope_tables():
    t = np.arange(T)
    row = (t // 64).astype(np.float32)
    col = (t % 64).astype(np.float32)
    inv = (10000.0 ** (-np.arange(32, dtype=np.float32) / 32)).astype(np.float32)
    cos = np.zeros((128, T), np.float32)
    sin = np.zeros((128, T), np.float32)
    for a, pos in enumerate((row, col)):
        ang = pos[None, :] * inv[:, None]
        for half in range(2):
            cos[a * 64 + half * 32:a * 64 + half * 32 + 32] = np.cos(ang)
            sin[a * 64 + half * 32:a * 64 + half * 32 + 32] = np.sin(ang)
    return cos, sin


def _consts(is_sample):
    cplan, NCST = cst_plan()
    C = np.zeros((128, NCST), np.float32)

    def put(name, a):
        o, w = cplan[name]
        C[:, o:o + w] = a

    idx = np.arange(128)
    k = idx[:, None]
    c = idx[None, :]
    put("ident", np.eye(128, dtype=np.float32))
    put("ones", np.ones((128, 128), np.float32))
    put("triF", (k <= c).astype(np.float32))
    put("triB", (k >= c).astype(np.float32))
    put("MiF", (c <= k).astype(np.float32))
    put("MsF", (c < k).astype(np.float32))
    put("MiB", (c >= k).astype(np.float32))
    put("MsB", (c > k).astype(np.float32))
    Rm = np.zeros((128, 128), np.float32)
    for a in range(2):
        for f in range(32):
            Rm[a * 64 + 32 + f, a * 64 + f] = -1.0
            Rm[a * 64 + f, a * 64 + 32 + f] = 1.0
    put("Rm", Rm)
    biasB = np.zeros((16, 7), np.float32)
    biasC = np.zeros((16, 20), np.float32)
    if is_sample:
        cos, sin = _rope_tables()
        put("mPT", (k >= c).astype(np.float32))
        put("mNT", (k <= c).astype(np.float32))
        flag, nflag = 1.0, 0.0
    else:
        cos = np.ones((128, T), np.float32)
        sin = np.zeros((128, T), np.float32)
        put("mPT", np.ones((128, 128), np.float32))
        put("mNT", np.ones((128, 128), np.float32))
        biasB[:, 0:4] = NEGB
        biasC[:, 0:4] = NEGB
        for n in range(16):
            if n % 2 == 0:
                biasB[n, 4] = NEGB
            else:
                biasB[n, 6] = NEGB
            for m in range(16):
                if m // 2 != n // 2:
                    biasC[n, 4 + m] = NEGB
        flag, nflag = 0.0, 1.0
    put("cos", cos)
    put("sin", sin)
    put("biasB", np.broadcast_to(biasB.reshape(1, -1), (128, 112)))
    put("biasC", np.broadcast_to(biasC.reshape(1, -1), (128, 320)))
    put("flag", flag)
    put("nflag", nflag)
    return C


def _col8(v):
    return np.ascontiguousarray(v.reshape(8, 128).T)


_NC_CACHE = {}


def kernel(x_prompt, x_sample, state_a, cache_b_kv, cache_c_kv, c, c_ctx, ada_w, ada_b, norm1_g, norm2_g,
           final_g, mlp_w1, mlp_w2, ev_w_in, a_conv, a_log, a_dt_bias, a_norm_g, b_sink, ev_w_out,
           od_w_in, c_qnorm_g, c_knorm_g, od_w_out):
    f = lambda a: np.asarray(a, dtype=np.float32)
    x_prompt, x_sample, state_a, cache_b_kv, cache_c_kv = map(f, (x_prompt, x_sample, state_a, cache_b_kv, cache_c_kv))
    c, c_ctx, ada_w, ada_b, norm1_g, norm2_g, final_g = map(f, (c, c_ctx, ada_w, ada_b, norm1_g, norm2_g, final_g))
    mlp_w1, mlp_w2, ev_w_in, a_conv, a_log, a_dt_bias, a_norm_g, b_sink, ev_w_out = map(
        f, (mlp_w1, mlp_w2, ev_w_in, a_conv, a_log, a_dt_bias, a_norm_g, b_sink, ev_w_out))
    od_w_in, c_qnorm_g, c_knorm_g, od_w_out = map(f, (od_w_in, c_qnorm_g, c_knorm_g, od_w_out))

    wplan, NPIECE = weight_plan()
    pplan, NPRM = prm_plan()
    wts = np.empty((NPIECE, 128, 2048), np.float32)

    def putw(key, arr):
        p0, n = wplan[key]
        assert arr.shape[0] == n, (key, arr.shape, n)
        wts[p0:p0 + n] = arr

    for l in range(DEPTH):
        putw(("ada", l), _pack(ada_w[l], 8, 256))
        putw(("w1", l), _pack(mlp_w1[l], 8, 256))
        putw(("w2", l), _pack(mlp_w2[l], 4, 512))
    for i in range(2):
        putw(("evA", i), _pack(ev_w_in[i][:, 0:2048], 8, 256))
        putw(("evB", i), _pack(ev_w_in[i][:, 2064:3088], 8, 256))
        putw(("evO", i), _pack(ev_w_out[i], 8, 256))
        putw(("odI", i), _pack(od_w_in[i], 8, 256))
        putw(("odO", i), _pack(od_w_out[i], 8, 256))
    wbg = np.stack([_pack(ev_w_in[i][:, 2048:2064], 8, 16)[0] for i in range(2)])

    def prm_for(cond):
        Pm = np.zeros((128, NPRM), np.float32)

        def put(name, a):
            o, w = pplan[name]
            Pm[:, o:o + w] = a

        put("cond", _col8(cond))
        for l in range(DEPTH):
            put(("ada_b", l), np.ascontiguousarray(ada_b[l].reshape(48, 128).T))
            put(("n1g", l), _col8(norm1_g[l]))
            put(("n2g", l), _col8(norm2_g[l]))
        put("fing", _col8(final_g))
        for i in range(2):
            cw = a_conv[i].reshape(3, 12, 128).transpose(2, 1, 0).reshape(128, 36)
            put(("conv", i), cw)
            put(("alog", i), np.broadcast_to(a_log[i].reshape(1, 8), (128, 8)))
            put(("dtb", i), np.broadcast_to(a_dt_bias[i].reshape(1, 8), (128, 8)))
            put(("ang", i), a_norm_g[i].reshape(128, 1))
            put(("sink", i), np.broadcast_to(b_sink[i].reshape(1, 4), (128, 4)))
            put(("cqg", i), c_qnorm_g[i].reshape(128, 1))
            put(("ckg", i), c_knorm_g[i].reshape(128, 1))
            put(("ckgb", i), np.broadcast_to(c_knorm_g[i].reshape(1, 128), (128, 128)))
        return Pm

    cst_s = _consts(True)
    cst_p = _consts(False)
    zeros_s0 = np.zeros((2, 2, 4, 128, 128), np.float32)
    zeros_ck = np.zeros((2, 2, 128, 512), np.float32)
    zeros_cv = np.zeros((2, 512, 256), np.float32)
    in_maps = []
    for core in range(8):
        if core < 4:
            b = core
            m = {
                "xT": np.ascontiguousarray(x_sample[b].T),
                "prm": prm_for(c[b]),
                "cst": cst_s,
                "s0": np.ascontiguousarray(state_a[b]),
                "ckb": np.ascontiguousarray(cache_b_kv[b, :, 0].transpose(0, 2, 3, 1)),
                "cvb": np.ascontiguousarray(cache_b_kv[b, :, 1].reshape(2, 512, 256)),
                "ckc": np.ascontiguousarray(cache_c_kv[b, :, 0].transpose(0, 2, 3, 1)),
                "cvc": np.ascontiguousarray(cache_c_kv[b, :, 1].reshape(2, 512, 256)),
            }
        else:
            pcid = (core - 4) % 2
            xs = x_prompt[pcid * 8:(pcid + 1) * 8].reshape(T, D)
            m = {
                "xT": np.ascontiguousarray(xs.T),
                "prm": prm_for(c_ctx),
                "cst": cst_p,
                "s0": zeros_s0, "ckb": zeros_ck, "cvb": zeros_cv, "ckc": zeros_ck, "cvc": zeros_cv,
            }
        m["wts"] = wts
        m["wbg"] = wbg
        in_maps.append(m)

    if "nc" not in _NC_CACHE:
        _NC_CACHE["nc"] = build_program()
    nc = _NC_CACHE["nc"]
    res = run_bass_kernel_spmd(nc, in_maps, core_ids=list(range(8)))
    R = res.results
    y_sample = np.stack([R[b]["yT"].T for b in range(4)])
    y_prompt = np.concatenate([R[4 + p]["yT"].T.reshape(8, 256, D) for p in range(2)])
    new_a = np.concatenate([R[4 + p]["ost"].transpose(1, 0, 2, 3, 4, 5) for p in range(2)])
    nb_k = np.concatenate([R[4 + p]["obk"].transpose(3, 0, 1, 2).reshape(8, 256, 2, 2, 128).transpose(0, 2, 1, 3, 4)
                           for p in range(2)])
    nb_v = np.concatenate([R[4 + p]["obv"].reshape(2, 8, 256, 2, 128).transpose(1, 0, 2, 3, 4) for p in range(2)])
    new_b = np.stack([nb_k, nb_v], axis=2)
    nc_k = np.concatenate([R[4 + p]["ock"].reshape(2, 8, 256, 2, 128).transpose(1, 0, 2, 3, 4) for p in range(2)])
    nc_v = np.concatenate([R[4 + p]["ocv"].reshape(2, 8, 256, 2, 128).transpose(1, 0, 2, 3, 4) for p in range(2)])
    new_c = np.stack([nc_k, nc_v], axis=2)
    return (np.ascontiguousarray(y_prompt, dtype=np.float32), np.ascontiguousarray(y_sample, dtype=np.float32),
            np.ascontiguousarray(new_a, dtype=np.float32), np.ascontiguousarray(new_b, dtype=np.float32),
            np.ascontiguousarray(new_c, dtype=np.float32))
```
